# Optimizing a Trainium2 kernel written in Bass

```python
import math
import jax, jax.numpy as jnp
from jax import lax
import numpy as np

D_MODEL = 1024
BATCH = 16
SEQ = 4096
DEPTH = 2
DEC_BATCH = 8
DEC_SEQ = 64
PAST_LEN = 1024

CHUNK = 64
N_MIXERS = 2
N_SSD_LAYERS = (DEPTH + 1) // 2
N_ATTN_LAYERS = DEPTH // 2
NORM_EPS = 1e-6

SSD_EXPAND = 2
D_INNER = SSD_EXPAND * D_MODEL
SSD_HEAD_DIM = 64
SSD_HEADS = D_INNER // SSD_HEAD_DIM
SSD_GROUPS = 4
SSD_HPG = SSD_HEADS // SSD_GROUPS
D_STATE = 128
CONV_W = 4
CONV_DIM = D_INNER + 2 * SSD_GROUPS * D_STATE
SSD_IN_DIM = D_INNER + CONV_DIM + SSD_HEADS

ATT_HEADS = 8
ATT_HEAD_DIM = D_MODEL // ATT_HEADS // 2
ATT_QK_DIM = ATT_HEADS * 2 * ATT_HEAD_DIM
ATT_V_DIM = ATT_HEADS * 2 * ATT_HEAD_DIM
ATT_QKV_DIM = 2 * ATT_QK_DIM + ATT_V_DIM
ROPE_THETA = 10000.0
Q_BLOCK = 128

D_FF = ((-(-8 * D_MODEL // 3)) + 255) // 256 * 256

kernel_name = "hybrid_ssd_diffattn_streaming_step"


def rmsnorm(x, w):
    xf = x.astype(jnp.float32)
    y = xf * lax.rsqrt(jnp.mean(xf * xf, axis=-1, keepdims=True) + NORM_EPS)
    return (y * w.astype(jnp.float32)).astype(x.dtype)


def swiglu(x, w_gate, w_up, w_down):
    return (jax.nn.silu(x @ w_gate) * (x @ w_up)) @ w_down


def rope(x, pos):
    d = x.shape[-1]
    half = d // 2
    inv = 1.0 / (ROPE_THETA ** (jnp.arange(half, dtype=jnp.float32) / half))
    ang = pos.astype(jnp.float32)[:, None] * inv[None, :]
    cos = jnp.cos(ang)[None, :, None, None, :]
    sin = jnp.sin(ang)[None, :, None, None, :]
    xf = x.astype(jnp.float32)
    x1, x2 = xf[..., :half], xf[..., half:]
    return jnp.concatenate([x1 * cos - x2 * sin, x2 * cos + x1 * sin], axis=-1).astype(x.dtype)


def causal_conv(xbc, hist, w, b):
    L = xbc.shape[1]
    xp = jnp.concatenate([hist.astype(xbc.dtype), xbc], axis=1)
    y = b + xp[:, 0:L] * w[0]
    for tap in range(1, CONV_W):
        y = y + xp[:, tap:tap + L] * w[tap]
    return jax.nn.silu(y), xp[:, -(CONV_W - 1):]


def ssd_scan(x, dt, A, Bm, Cm, h0):
    b, L = x.shape[0], x.shape[1]
    q = min(CHUNK, L)
    nc = L // q

    def to_chunks(t):
        return jnp.moveaxis(t.reshape((b, nc, q) + t.shape[2:]), 1, 0)

    mask = jnp.tril(jnp.ones((q, q), dtype=bool))[None, :, :, None, None]

    def step(h, inp):
        xc, dtc, bc, cc = inp
        xc = xc.astype(jnp.float32)
        bc = bc.astype(jnp.float32)
        cc = cc.astype(jnp.float32)
        acum = jnp.cumsum(dtc * A, axis=1)
        seg = acum[:, :, None] - acum[:, None, :]
        lmat = jnp.exp(jnp.where(mask, seg, -jnp.inf))
        dx = dtc[..., None] * xc
        cb = jnp.einsum('bign,bjgn->bgij', cc, bc)
        y_diag = jnp.einsum('bgij,bijgr,bjgrp->bigrp', cb, lmat, dx)
        y_off = jnp.einsum('bign,bgrpn->bigrp', cc, h) * jnp.exp(acum)[..., None]
        wdec = jnp.exp(acum[:, -1:] - acum)[..., None] * dx
        h_new = h * jnp.exp(acum[:, -1])[..., None, None] + jnp.einsum('bjgn,bjgrp->bgrpn', bc, wdec)
        return h_new, y_diag + y_off

    hT, ys = lax.scan(step, h0, (to_chunks(x), to_chunks(dt), to_chunks(Bm), to_chunks(Cm)))
    y = jnp.moveaxis(ys, 0, 1).reshape(x.shape)
    return y, hT


def ssd_mixer(xn, conv_hist, h0, in_proj, conv_w, conv_b, dt_bias, a_log, d_skip, norm_w, out_proj):
    b, L, _ = xn.shape
    zxbcdt = xn @ in_proj
    z = zxbcdt[..., :D_INNER]
    xbc = zxbcdt[..., D_INNER:D_INNER + CONV_DIM]
    dt = zxbcdt[..., D_INNER + CONV_DIM:]
    xbc, new_hist = causal_conv(xbc, conv_hist, conv_w, conv_b)
    xs = xbc[..., :D_INNER].reshape(b, L, SSD_GROUPS, SSD_HPG, SSD_HEAD_DIM)
    Bm = xbc[..., D_INNER:D_INNER + SSD_GROUPS * D_STATE].reshape(b, L, SSD_GROUPS, D_STATE)
    Cm = xbc[..., D_INNER + SSD_GROUPS * D_STATE:].reshape(b, L, SSD_GROUPS, D_STATE)
    dt = jax.nn.softplus(dt.astype(jnp.float32) + dt_bias.astype(jnp.float32)).reshape(b, L, SSD_GROUPS, SSD_HPG)
    A = -jnp.exp(a_log.astype(jnp.float32)).reshape(SSD_GROUPS, SSD_HPG)
    y, hT = ssd_scan(xs, dt, A, Bm, Cm, h0)
    y = y + d_skip.astype(jnp.float32).reshape(SSD_GROUPS, SSD_HPG)[..., None] * xs.astype(jnp.float32)
    y = y.reshape(b, L, D_INNER) * jax.nn.silu(z.astype(jnp.float32))
    y = rmsnorm(y, norm_w).astype(xn.dtype)
    return y @ out_proj, new_hist, hT


def diff_qkv(xn, pos, w_qkv):
    b, L, _ = xn.shape
    qkv = xn @ w_qkv
    q = rope(qkv[..., :ATT_QK_DIM].reshape(b, L, ATT_HEADS, 2, ATT_HEAD_DIM), pos)
    k = rope(qkv[..., ATT_QK_DIM:2 * ATT_QK_DIM].reshape(b, L, ATT_HEADS, 2, ATT_HEAD_DIM), pos)
    v = qkv[..., 2 * ATT_QK_DIM:].reshape(b, L, ATT_HEADS, 2 * ATT_HEAD_DIM)
    return q, k, v


def diff_core(q, k, v, q_pos, k_pos, lam):
    scale = ATT_HEAD_DIM ** -0.5
    s = jnp.einsum('bqhcd,bkhcd->bhcqk', q, k).astype(jnp.float32) * scale
    mask = (k_pos[None, :] // CHUNK) <= (q_pos[:, None] // CHUNK)
    s = jnp.where(mask, s, -jnp.inf)
    p = jax.nn.softmax(s, axis=-1)
    a = p[:, :, 0] - lam * p[:, :, 1]
    return jnp.einsum('bhqk,bkhe->bqhe', a, v.astype(jnp.float32))


def diff_out(o, subln_w, lambda_init, w_o, dtype):
    b, L = o.shape[0], o.shape[1]
    o = rmsnorm(o, subln_w) * (1.0 - lambda_init)
    return o.reshape(b, L, ATT_V_DIM).astype(dtype) @ w_o


def setup_inputs(seed: int = 0) -> dict:
    key = jax.random.key(seed)
    ks = jax.random.split(key, 32)
    f32 = jnp.float32
    nrm = lambda k, shape, s: jax.random.normal(k, shape, f32) * s
    dt0 = jnp.exp(jax.random.uniform(ks[10], (N_SSD_LAYERS, SSD_HEADS), f32, math.log(1e-3), math.log(1e-1)))
    return {
        "x_prompt": nrm(ks[0], (BATCH, SEQ, D_MODEL), 1.0),
        "x_sample": nrm(ks[1], (DEC_BATCH, DEC_SEQ, D_MODEL), 1.0),
        "state_conv": nrm(ks[2], (N_SSD_LAYERS, DEC_BATCH, CONV_W - 1, CONV_DIM), 1.0),
        "state_ssm": nrm(ks[3], (N_SSD_LAYERS, DEC_BATCH, SSD_HEADS, SSD_HEAD_DIM, D_STATE), 0.5),
        "cache_k": nrm(ks[4], (N_ATTN_LAYERS, DEC_BATCH, PAST_LEN, ATT_HEADS, 2, ATT_HEAD_DIM), 1.0),
        "cache_v": nrm(ks[5], (N_ATTN_LAYERS, DEC_BATCH, PAST_LEN, ATT_HEADS, 2 * ATT_HEAD_DIM), 1.0),
        "norm_mix": 1.0 + nrm(ks[6], (DEPTH, D_MODEL), 0.02),
        "norm_ffn": 1.0 + nrm(ks[7], (DEPTH, D_MODEL), 0.02),
        "norm_final": 1.0 + nrm(ks[8], (D_MODEL,), 0.02),
        "ssd_in_proj": nrm(ks[9], (N_SSD_LAYERS, D_MODEL, SSD_IN_DIM), D_MODEL ** -0.5),
        "ssd_conv_w": nrm(ks[11], (N_SSD_LAYERS, CONV_W, CONV_DIM), CONV_W ** -0.5),
        "ssd_conv_b": nrm(ks[12], (N_SSD_LAYERS, CONV_DIM), 0.02),
        "ssd_dt_bias": dt0 + jnp.log(-jnp.expm1(-dt0)),
        "ssd_a_log": jnp.log(jax.random.uniform(ks[13], (N_SSD_LAYERS, SSD_HEADS), f32, 1.0, 16.0)),
        "ssd_d": 1.0 + nrm(ks[14], (N_SSD_LAYERS, SSD_HEADS), 0.02),
        "ssd_norm": 1.0 + nrm(ks[15], (N_SSD_LAYERS, D_INNER), 0.02),
        "ssd_out_proj": nrm(ks[16], (N_SSD_LAYERS, D_INNER, D_MODEL), D_INNER ** -0.5),
        "attn_qkv": nrm(ks[17], (N_ATTN_LAYERS, D_MODEL, ATT_QKV_DIM), D_MODEL ** -0.5),
        "attn_lambda_q1": nrm(ks[18], (N_ATTN_LAYERS, ATT_HEAD_DIM), 0.1),
        "attn_lambda_k1": nrm(ks[19], (N_ATTN_LAYERS, ATT_HEAD_DIM), 0.1),
        "attn_lambda_q2": nrm(ks[20], (N_ATTN_LAYERS, ATT_HEAD_DIM), 0.1),
        "attn_lambda_k2": nrm(ks[21], (N_ATTN_LAYERS, ATT_HEAD_DIM), 0.1),
        "attn_subln": 1.0 + nrm(ks[22], (N_ATTN_LAYERS, 2 * ATT_HEAD_DIM), 0.02),
        "attn_out": nrm(ks[23], (N_ATTN_LAYERS, ATT_V_DIM, D_MODEL), ATT_V_DIM ** -0.5),
        "ffn_gate": nrm(ks[24], (DEPTH, D_MODEL, D_FF), D_MODEL ** -0.5),
        "ffn_up": nrm(ks[25], (DEPTH, D_MODEL, D_FF), D_MODEL ** -0.5),
        "ffn_down": nrm(ks[26], (DEPTH, D_FF, D_MODEL), D_FF ** -0.5),
    }


def reference(x_prompt, x_sample, state_conv, state_ssm, cache_k, cache_v,
              norm_mix, norm_ffn, norm_final,
              ssd_in_proj, ssd_conv_w, ssd_conv_b, ssd_dt_bias, ssd_a_log, ssd_d, ssd_norm, ssd_out_proj,
              attn_qkv, attn_lambda_q1, attn_lambda_k1, attn_lambda_q2, attn_lambda_k2, attn_subln, attn_out,
              ffn_gate, ffn_up, ffn_down):
    bp, lp = x_prompt.shape[0], x_prompt.shape[1]
    bs, ls = x_sample.shape[0], x_sample.shape[1]
    past = cache_k.shape[2]
    pos_p = jnp.arange(lp)
    pos_s = past + jnp.arange(ls)
    kpos_s = jnp.arange(past + ls)
    xp, xs = x_prompt, x_sample
    conv_p, ssm_p, k_p, v_p = [], [], [], []
    conv_s, ssm_s, k_s, v_s = [], [], [], []

    for i in range(DEPTH):
        hp = rmsnorm(xp, norm_mix[i])
        hs = rmsnorm(xs, norm_mix[i])
        j = i // N_MIXERS
        if i % N_MIXERS == 0:
            params = (ssd_in_proj[j], ssd_conv_w[j], ssd_conv_b[j], ssd_dt_bias[j], ssd_a_log[j],
                      ssd_d[j], ssd_norm[j], ssd_out_proj[j])
            hist0 = jnp.zeros((bp, CONV_W - 1, CONV_DIM), xp.dtype)
            h0 = jnp.zeros((bp, SSD_GROUPS, SSD_HPG, SSD_HEAD_DIM, D_STATE), jnp.float32)
            yp, hist_p, hT_p = ssd_mixer(hp, hist0, h0, *params)
            h0s = state_ssm[j].astype(jnp.float32).reshape(bs, SSD_GROUPS, SSD_HPG, SSD_HEAD_DIM, D_STATE)
            ys, hist_s, hT_s = ssd_mixer(hs, state_conv[j], h0s, *params)
            conv_p.append(hist_p)
            ssm_p.append(hT_p.reshape(bp, SSD_HEADS, SSD_HEAD_DIM, D_STATE))
            conv_s.append(hist_s)
            ssm_s.append(hT_s.reshape(bs, SSD_HEADS, SSD_HEAD_DIM, D_STATE))
        else:
            lambda_init = 0.8 - 0.6 * math.exp(-0.3 * i)
            lam = (jnp.exp(jnp.sum(attn_lambda_q1[j].astype(jnp.float32) * attn_lambda_k1[j].astype(jnp.float32)))
                   - jnp.exp(jnp.sum(attn_lambda_q2[j].astype(jnp.float32) * attn_lambda_k2[j].astype(jnp.float32)))
                   + lambda_init)
            qp, kp, vp = diff_qkv(hp, pos_p, attn_qkv[j])
            nb = lp // Q_BLOCK
            qb = jnp.moveaxis(qp.reshape(bp, nb, Q_BLOCK, ATT_HEADS, 2, ATT_HEAD_DIM), 1, 0)
            pb = pos_p.reshape(nb, Q_BLOCK)
            ob = lax.map(lambda a: diff_core(a[0], kp, vp, a[1], pos_p, lam), (qb, pb))
            op = jnp.moveaxis(ob, 0, 1).reshape(bp, lp, ATT_HEADS, 2 * ATT_HEAD_DIM)
            yp = diff_out(op, attn_subln[j], lambda_init, attn_out[j], xp.dtype)
            qs, ks_new, vs_new = diff_qkv(hs, pos_s, attn_qkv[j])
            k_all = jnp.concatenate([cache_k[j].astype(ks_new.dtype), ks_new], axis=1)
            v_all = jnp.concatenate([cache_v[j].astype(vs_new.dtype), vs_new], axis=1)
            os_ = diff_core(qs, k_all, v_all, pos_s, kpos_s, lam)
            ys = diff_out(os_, attn_subln[j], lambda_init, attn_out[j], xs.dtype)
            k_p.append(kp)
            v_p.append(vp)
            k_s.append(ks_new)
            v_s.append(vs_new)
        xp = xp + yp
        xs = xs + ys
        xp = xp + swiglu(rmsnorm(xp, norm_ffn[i]), ffn_gate[i], ffn_up[i], ffn_down[i])
        xs = xs + swiglu(rmsnorm(xs, norm_ffn[i]), ffn_gate[i], ffn_up[i], ffn_down[i])

    y_prompt = rmsnorm(xp, norm_final)
    y_sample = rmsnorm(xs, norm_final)
    new_conv_prompt = jnp.stack(conv_p)
    new_ssm_prompt = jnp.stack(ssm_p)
    new_k_prompt = jnp.stack(k_p)
    new_v_prompt = jnp.stack(v_p)
    new_conv_sample = jnp.stack(conv_s)
    new_ssm_sample = jnp.stack(ssm_s)
    new_k_sample = jnp.stack(k_s)
    new_v_sample = jnp.stack(v_s)
    return (y_prompt, y_sample, new_conv_prompt, new_ssm_prompt, new_k_prompt, new_v_prompt,
            new_conv_sample, new_ssm_sample, new_k_sample, new_v_sample)
```

```python
import contextlib
import math
import os
import numpy as np
import concourse.bass as bass
import concourse.mybir as mybir
from concourse.bass_utils import run_bass_kernel_spmd

F32 = mybir.dt.float32
BF16 = mybir.dt.bfloat16
AF = mybir.ActivationFunctionType
ALU = mybir.AluOpType

NBLK = 61
EPS = 1e-6
LAMBDA_INIT = 0.8 - 0.6 * math.exp(-0.3 * 1)
NSLOT = 3

PP_NW = 0
PP_CONVW = 40
PP_CONVB = 136
PP_DTB = 160
PP_ALOG = 192
PP_DSK = 224
PP_SSDNW = 256
PP_LAM = 272
PP_SUBW = 528
NPP = 532


class Stream:
    def __init__(self, sem, step, name):
        self.sem = sem
        self.step = step
        self.cnt = 0
        self.name = name


class Buf:
    def __init__(self, name, stream=None):
        self.name = name
        self.ws = {}
        self.rs = {}
        self.stream = stream


class Eng:
    def __init__(self, h, stream):
        self.h = h
        self.st = stream
        self.seen = {}


class FW:
    def __init__(self, nc, es):
        self.nc = nc
        self.es = es
        self.nsem = 0
        self.pe = Eng(nc.tensor, self.new_stream("pe", 1))
        self.act = Eng(nc.scalar, self.new_stream("act", 1))
        self.dve = Eng(nc.vector, self.new_stream("dve", 1))
        self.pool = Eng(nc.gpsimd, self.new_stream("pool", 1))
        self.sp = Eng(nc.sync, self.new_stream("sp", 1))
        self.dma_streams = []
        self.named = {}

    def new_stream(self, name, step):
        sem = self.es.enter_context(self.nc.semaphore(f"s{self.nsem}_{name}"))
        self.nsem += 1
        return Stream(sem, step, name)

    def buf(self, name, dma=False):
        st = None
        if dma:
            st = self.named.get(name)
            if st is None:
                st = self.new_stream(name, 16)
                self.named[name] = st
                self.dma_streams.append(st)
        return Buf(name, st)

    def _deps(self, r, w, wa):
        deps = {}

        def add(d):
            for s, c in d.items():
                if deps.get(s, 0) < c:
                    deps[s] = c
        for b in r:
            add(b.ws)
        for b in w:
            add(b.ws)
            add(b.rs)
        for b in wa:
            add(b.rs)
        return deps

    def _wait(self, eng, deps):
        for s, c in deps.items():
            if eng.seen.get(s, 0) < c:
                eng.h.wait_ge(s.sem, c * s.step)
                eng.seen[s] = c

    def _commit(self, st, r, w, wa):
        c = st.cnt
        for b in r:
            if b.rs.get(st, 0) < c:
                b.rs[st] = c
        for b in w:
            b.ws = {st: c}
            b.rs = {}
        for b in wa:
            if b.ws.get(st, 0) < c:
                b.ws[st] = c

    def op(self, eng, fn, r=(), w=(), wa=()):
        self._wait(eng, self._deps(r, w, wa))
        ins = fn()
        eng.st.cnt += 1
        ins.then_inc(eng.st.sem, 1)
        self._commit(eng.st, r, w, wa)

    def dma(self, q, out, in_, slot, r=(), w=(), wa=(), **kw):
        st = slot.stream
        self._wait(q, self._deps(r, w, wa))
        ins = q.h.dma_start(out=out, in_=in_, **kw)
        st.cnt += 1
        ins.then_inc(st.sem, 16)
        self._commit(st, r, w, wa)

    def barrier(self, streams=()):
        engs = [self.pe, self.act, self.dve, self.pool]
        deps = {e.st: e.st.cnt for e in engs if e.st.cnt > 0}
        for s in streams:
            if s.cnt > 0:
                deps[s] = s.cnt
        for e in engs:
            self._wait(e, {s: c for s, c in deps.items() if s is not e.st})

    def finish(self, eng):
        deps = {s: s.cnt for s in self.dma_streams if s.cnt > 0}
        for e in [self.pe, self.act, self.dve, self.pool]:
            if e.st.cnt > 0:
                deps[e.st] = e.st.cnt
        self._wait(eng, deps)


def build_program(L, LS, PAST, T):
    nc = bass.Bass("TRN2", target_bir_lowering=False)
    V, A, G, PE = nc.vector, nc.scalar, nc.gpsimd, nc.tensor
    NT = L // T
    LK = max(L, PAST + LS)
    NBK = (LK + 127) // 128

    def din(n, s):
        return nc.dram_tensor(n, s, F32, kind="ExternalInput").ap()

    def dout(n, s):
        return nc.dram_tensor(n, s, F32, kind="ExternalOutput").ap()

    xp = din("xp", [2, L, 1024])
    xs_d = din("xs", [LS, 1024])
    sconv = din("sconv", [3, 3072])
    sssm = din("sssm", [2048, 128])
    ck = din("ck", [PAST, 1024])
    cv = din("cv", [PAST, 1024])
    wst = din("wst", [NBLK, 128, 4096])
    cst = din("cst", [128, 512])
    ropeP = din("ropeP", [L, 128])
    ropeS = din("ropeS", [LS, 128])
    pp_d = din("pp", [128, NPP])
    o_yp = dout("o_yp", [2, L, 1024])
    o_ys = dout("o_ys", [LS, 1024])
    o_convp = dout("o_convp", [2, 3, 3072])
    o_ssmp = dout("o_ssmp", [2, 2048, 128])
    o_kp = dout("o_kp", [2, L, 1024])
    o_vp = dout("o_vp", [2, L, 1024])
    o_convs = dout("o_convs", [3, 3072])
    o_ssms = dout("o_ssms", [2048, 128])
    o_ks = dout("o_ks", [LS, 1024])
    o_vs = dout("o_vs", [LS, 1024])
    wbf = nc.dram_tensor("wbf", [NBLK, 128, 4096], BF16, kind="Internal").ap()
    KT = nc.dram_tensor("KTs", [3, 8, 128, LK], BF16, kind="Internal").ap()
    VS = nc.dram_tensor("VSs", [3, 8, 128, NBK, 128], BF16, kind="Internal").ap()

    with contextlib.ExitStack() as es:
        fw = FW(nc, es)
        pe, act, dve, pool, sp = fw.pe, fw.act, fw.dve, fw.pool, fw.sp

        uniq = [0]

        def sb(n, s, d=F32, st=es):
            uniq[0] += 1
            return st.enter_context(nc.sbuf_tensor(f"{n}_{uniq[0]}", s, d))

        cst_t = sb("cst_t", [128, 512])
        ident = cst_t[:, 0:128]
        Um = cst_t[:, 128:256]
        Lm = cst_t[:, 256:384]
        ones_f = cst_t[:, 384:512]
        identb = sb("identb", [128, 128], BF16)
        ones_b = sb("ones_b", [128, 128], BF16)
        pp = sb("pp_t", [128, NPP])
        smallp = sb("smallp", [128, 64])
        xT = sb("xT", [128, 8, T])
        xnT = sb("xnT", [128, 8, T], BF16)
        wring = [sb(f"wring{i}", [128, 4096], BF16) for i in range(NSLOT)]
        xin = sb("xin", [128, 1024])
        sqr = [sb(f"sqr{i}", [128, T], BF16) for i in range(2)]
        rt = sb("rt", [128, T])
        rstd = sb("rstd", [128, T])
        hst = sb("hst", [128, 2048])
        hbf = sb("hbf", [128, 2048], BF16)
        hist = sb("hist", [128, 24, 3])
        ps = [es.enter_context(nc.psum_tensor(f"ps{i}", [128, 512], F32)) for i in range(8)]

        B_cst = fw.buf("cst", dma=True)
        B_pp = fw.buf("pp", dma=True)
        B_identb = fw.buf("identb")
        B_small = fw.buf("small")
        B_xT = fw.buf("xT")
        B_xnT = fw.buf("xnT")
        B_ring = [fw.buf(f"ring{i}", dma=True) for i in range(NSLOT)]
        B_xin = fw.buf("xin", dma=True)
        B_sqr = [fw.buf(f"sqr{i}") for i in range(2)]
        B_rt = fw.buf("rt")
        B_rstd = fw.buf("rstd")
        B_h = [fw.buf(f"h{g}") for g in range(4)]
        B_hbf = [fw.buf(f"hbf{g}") for g in range(4)]
        B_hist = fw.buf("hist", dma=True)
        PB = [fw.buf(f"psb{i}") for i in range(8)]
        B_pro = fw.buf("pro", dma=True)
        B_wbf = fw.buf("wbf")
        B_KT = fw.buf("KT")
        B_VS = fw.buf("VS")
        pstate = {"bank": 0, "g": 0, "nbanks": 8}

        def nb():
            i = pstate["bank"]
            pstate["bank"] = (i + 1) % pstate["nbanks"]
            return i % pstate["nbanks"]

        def bc(ap, axis, shape):
            return ap.unsqueeze(axis).to_broadcast(shape)

        fw.dma(sp, cst_t[:], cst, B_cst, w=[B_cst])
        fw.dma(sp, pp[:], pp_d, B_pp, w=[B_pp])
        PGRP = [0] * 11 + [1] * 23 + [2] * 8 + [3] * 19
        B_prog = [fw.buf(f"pro{i}", dma=True) for i in range(4)]
        B_wbfg = [fw.buf(f"wbf{i}") for i in range(4)]
        for b in range(NBLK):
            fw.dma(pool, wbf[b], wst[b], B_prog[PGRP[b]], wa=[B_wbfg[PGRP[b]]])
        fw.op(dve, lambda: V.tensor_copy(out=identb[:], in_=ident), r=[B_cst], w=[B_identb])
        fw.op(dve, lambda: V.tensor_copy(out=ones_b[:], in_=ones_f), r=[B_cst], wa=[B_identb])
        fw.op(act, lambda: A.activation(out=smallp[:, 0:32], in_=pp[:, PP_ALOG:PP_ALOG + 32], func=AF.Exp),
              r=[B_pp], w=[B_small])
        fw.op(dve, lambda: V.tensor_scalar(out=smallp[:, 0:32], in0=smallp[:, 0:32], scalar1=-1.0, scalar2=None,
                                           op0=ALU.mult), r=[B_small], w=[B_small])
        lq = pp[:, PP_LAM:PP_LAM + 256].rearrange("p (a b) -> p a b", a=4)
        lamt = sb("lamt", [128, 2, 64])
        B_lamt = fw.buf("lamt")
        fw.op(dve, lambda: V.tensor_tensor(out=lamt[:, 0, :], in0=lq[:, 0, :], in1=lq[:, 1, :], op=ALU.mult),
              r=[B_pp], w=[B_lamt])
        fw.op(dve, lambda: V.tensor_tensor(out=lamt[:, 1, :], in0=lq[:, 2, :], in1=lq[:, 3, :], op=ALU.mult),
              r=[B_pp, B_lamt], w=[B_lamt])
        fw.op(dve, lambda: V.tensor_reduce(out=smallp[:, 33:35], in_=lamt[:], axis=mybir.AxisListType.X, op=ALU.add),
              r=[B_lamt, B_small], w=[B_small])
        fw.op(act, lambda: A.activation(out=smallp[:, 35:37], in_=smallp[:, 33:35], func=AF.Exp), r=[B_small], w=[B_small])
        fw.op(dve, lambda: V.scalar_tensor_tensor(out=smallp[:, 32:33], in0=smallp[:, 36:37], scalar=-LAMBDA_INIT,
                                                  in1=smallp[:, 35:36], op0=ALU.add, op1=ALU.subtract),
              r=[B_small], w=[B_small])
        negl = smallp[:, 32:33]
        A_rep = smallp[:, 0:32]

        total_blocks = (2 * NT + 1) * NBLK

        def ring_issue(g):
            if g >= total_blocks or g >= int(os.environ.get('K_RING', '1000000')):
                return
            s = g % NSLOT
            fw.dma(sp, wring[s][:], wbf[g % NBLK], B_ring[s], r=[B_wbfg[PGRP[g % NBLK]]], w=[B_ring[s]])

        for g in range(NSLOT):
            ring_issue(g)

        class Blk:
            def __init__(self):
                self.g = pstate["g"]
                self.s = self.g % NSLOT
                self.t = wring[self.s]
                self.b = B_ring[self.s]

            def done(self):
                pstate["g"] += 1
                ring_issue(self.g + NSLOT)

        def rmsnorm_xT(widx, cw_tot, out_bf=True, out_ap=None):
            bk = nb()
            for kc in range(8):
                j = kc % 2
                fw.op(act, lambda: A.activation(out=sqr[j][:, :cw_tot], in_=xT[:, kc, :cw_tot], func=AF.Square),
                      r=[B_xT], w=[B_sqr[j]])
                fw.op(pe, lambda: PE.matmul(ps[bk][:, :cw_tot], ones_b[:], sqr[j][:, :cw_tot], start=(kc == 0), stop=(kc == 7)),
                      r=[B_sqr[j], B_identb], w=[PB[bk]] if kc == 0 else (), wa=[PB[bk]] if kc else ())
            fw.op(act, lambda: A.activation(out=rt[:, :cw_tot], in_=ps[bk][:, :cw_tot], func=AF.Sqrt, scale=1.0 / 1024, bias=EPS),
                  r=[PB[bk]], w=[B_rt])
            fw.op(dve, lambda: V.reciprocal(out=rstd[:, :cw_tot], in_=rt[:, :cw_tot]), r=[B_rt], w=[B_rstd])
            for kc in range(8):
                dst = xnT[:, kc, :cw_tot] if out_ap is None else out_ap[:, kc, :cw_tot]
                fw.op(dve, lambda: V.scalar_tensor_tensor(out=dst, in0=xT[:, kc, :cw_tot],
                                                          scalar=pp[:, PP_NW + widx * 8 + kc:PP_NW + widx * 8 + kc + 1],
                                                          in1=rstd[:, :cw_tot], op0=ALU.mult, op1=ALU.mult),
                      r=[B_xT, B_rstd, B_pp], wa=[B_xnT])

        def gemm_F(blk_view_fn, nk, rhs_fn, ntok, rbufs):
            bk = nb()

            def f():
                ins = None
                for kk in range(nk):
                    ins = PE.matmul(ps[bk][:, :ntok], blk_view_fn(kk), rhs_fn(kk), start=(kk == 0), stop=(kk == nk - 1))
                return ins
            fw.op(pe, f, r=rbufs, w=[PB[bk]])
            return bk

        def ffn(layer, Ttok, in_spec=None, out_spec=None):
            with contextlib.ExitStack() as ph:
                bst = bnd_prefetch(in_spec, ph)
                hT = sb("hT", [128, 22, T], BF16, ph)
                sg = [sb(f"sg{i}", [128, T], F32, ph) for i in range(2)]
                B_hT = fw.buf("hT")
                B_sg = [fw.buf(f"sg{i}") for i in range(2)]
                rmsnorm_xT(1 + 2 * layer, Ttok)
                for b in range(11):
                    blk = Blk()
                    wv = blk.t[:].rearrange("p (f u k c) -> p f u k c", f=2, u=2, k=8)
                    for f_ in range(2):
                        fc = 2 * b + f_
                        bg = gemm_F(lambda kk: wv[:, f_, 0, kk, :], 8, lambda kk: xnT[:, kk, :Ttok], Ttok, [blk.b, B_xnT])
                        bu = gemm_F(lambda kk: wv[:, f_, 1, kk, :], 8, lambda kk: xnT[:, kk, :Ttok], Ttok, [blk.b, B_xnT])
                        j = fc % 2
                        fw.op(act, lambda: A.activation(out=sg[j][:, :Ttok], in_=ps[bg][:, :Ttok], func=AF.Silu),
                              r=[PB[bg]], w=[B_sg[j]])
                        fw.op(dve, lambda: V.tensor_tensor(out=hT[:, fc, :Ttok], in0=sg[j][:, :Ttok], in1=ps[bu][:, :Ttok], op=ALU.mult),
                              r=[B_sg[j], PB[bu]], wa=[B_hT])
                    blk.done()
                for b in range(8):
                    blk = Blk()
                    wv = blk.t[:, 0:2816].rearrange("p (k c) -> p k c", k=22)
                    bk = gemm_F(lambda kk: wv[:, kk, :], 22, lambda kk: hT[:, kk, :Ttok], Ttok, [blk.b, B_hT])
                    fw.op(dve, lambda: V.tensor_tensor(out=xT[:, b, :Ttok], in0=xT[:, b, :Ttok], in1=ps[bk][:, :Ttok], op=ALU.add),
                          r=[PB[bk], B_xT], wa=[B_xT])
                    blk.done()
                streams = bnd_finish(out_spec, bst, ph) if (in_spec is not None or out_spec is not None) else []
                fw.barrier(streams)

        def load_x(x_rows_ap, Ttok, cw):
            nsub = Ttok // cw
            for s in range(nsub):
                fw.dma(sp, xin[:cw, :], x_rows_ap[s * cw:(s + 1) * cw, :], B_xin, w=[B_xin])
                for half in range(2):
                    bk = nb()

                    def f():
                        ins = None
                        for q in range(4):
                            kc = half * 4 + q
                            ins = PE.transpose(ps[bk][:, q * cw:(q + 1) * cw], xin[:cw, kc * 128:(kc + 1) * 128], ident[:cw, :cw])
                        return ins
                    fw.op(pe, f, r=[B_xin, B_cst], w=[PB[bk]])
                    fw.op(act, lambda: A.copy(out=xT[:, half * 4:half * 4 + 4, s * cw:(s + 1) * cw],
                                              in_=ps[bk][:, :4 * cw].rearrange("p (q c) -> p q c", q=4)),
                          r=[PB[bk], B_xT], wa=[B_xT])

        def layer0(Ttok, cw, seq):
            nsub = Ttok // cw
            with contextlib.ExitStack() as ph:
                zact = sb("zact", [128, nsub, 2048], BF16, ph)
                xstok = sb("xstok", [128, nsub, 2048], BF16, ph)
                bct = sb("bct", [128, 8, T], BF16, ph)
                cin = [sb(f"cin{i}", [128, T + 3], F32, ph) for i in range(2)]
                acc = [sb(f"acc{i}", [128, T], F32, ph) for i in range(2)]
                xsa = [sb(f"xsa{i}", [128, T], BF16, ph) for i in range(3)]
                dtt = sb("dtt", [128, nsub, 32], F32, ph)
                att = sb("att", [128, nsub, 32], F32, ph)
                sm2 = [sb(f"sm{i}", [128, 5, 32], F32, ph) for i in range(2)]
                rhsA = [sb(f"rhsA{i}", [128, 1024], F32, ph) for i in range(2)]
                Et = [sb(f"Et{i}", [128, 1024], BF16, ph) for i in range(2)]
                Mt = [sb(f"Mt{i}", [128, 1024], BF16, ph) for i in range(2)]
                dxt = [sb(f"dxt{i}", [128, 512], BF16, ph) for i in range(2)]
                wdt = [sb(f"wdt{i}", [128, 512], BF16, ph) for i in range(3)]
                cbm = [sb(f"cbm{i}", [128, 128], BF16, ph) for i in range(2)]
                t1 = [sb(f"t1{i}", [128, 512], F32, ph) for i in range(2)]
                t3 = [sb(f"t3{i}", [128, 512], F32, ph) for i in range(3)]
                hdec = [sb(f"hdec{i}", [128, 512], F32, ph) for i in range(2)]
                btok = [sb(f"btok{i}", [128, 128], BF16, ph) for i in range(3)]
                ytok = sb("ytok", [128, 2048], F32, ph)
                yn = sb("yn", [128, 2048], BF16, ph)
                ssq = sb("ssq", [128, 4], F32, ph)
                ynT = sb("ynT", [128, 16, T], BF16, ph)
                Bn = lambda n: fw.buf(n)
                B_zact, B_xstok, B_bct, B_dtt, B_att = Bn("zact"), Bn("xstok"), Bn("bct"), Bn("dtt"), Bn("att")
                B_sm2 = [Bn("sm0"), Bn("sm1")]
                B_cin = [Bn("cin0"), Bn("cin1")]
                B_acc = [Bn("acc0"), Bn("acc1")]
                B_xsa = [Bn("xsa0"), Bn("xsa1"), Bn("xsa2")]
                B_rhsA = [Bn("rhsA0"), Bn("rhsA1")]
                B_Et = [Bn("Et0"), Bn("Et1")]
                B_Mt = [Bn("Mt0"), Bn("Mt1")]
                B_dxt = [Bn("dx0"), Bn("dx1")]
                B_wdt = [Bn("wd0"), Bn("wd1"), Bn("wd2")]
                B_cbm = [Bn("cbm0"), Bn("cbm1")]
                B_t1 = [Bn("t10"), Bn("t11")]
                B_t3 = [Bn("t30"), Bn("t31"), Bn("t32")]
                B_hdec = [Bn("hdec0"), Bn("hdec1")]
                B_btok = [Bn("btok0"), Bn("btok1"), Bn("btok2")]
                B_ytok, B_yn, B_ssq, B_ynT = Bn("ytok"), Bn("yn"), Bn("ssq"), Bn("ynT")

                rmsnorm_xT(0, Ttok)
                for b in range(4):
                    blk = Blk()
                    wv = blk.t[:].rearrange("p (k c) -> p k c", k=8)
                    for s in range(nsub):
                        bk = nb()

                        def f():
                            ins = None
                            for kk in range(8):
                                ins = PE.matmul(ps[bk][:cw, :], xnT[:, kk, s * cw:(s + 1) * cw], wv[:, kk, :], start=(kk == 0), stop=(kk == 7))
                            return ins
                        fw.op(pe, f, r=[blk.b, B_xnT], w=[PB[bk]])
                        fw.op(act, lambda: A.activation(out=zact[:cw, s, b * 512:(b + 1) * 512], in_=ps[bk][:cw, :], func=AF.Silu),
                              r=[PB[bk]], wa=[B_zact])
                    blk.done()
                pend = []
                pend3 = []
                for b in range(6):
                    blk = Blk()
                    wv = blk.t[:].rearrange("p (k c) -> p k c", k=8)
                    for m in range(4):
                        ch = 4 * b + m
                        j = ch % 2
                        bk = gemm_F(lambda kk: wv[:, kk, m * 128:(m + 1) * 128], 8, lambda kk: xnT[:, kk, :Ttok], Ttok, [blk.b, B_xnT])
                        fw.op(pool, lambda: G.tensor_copy(out=cin[j][:, 0:3], in_=hist[:, ch, :]), r=[B_hist], w=[B_cin[j]])
                        fw.op(act, lambda: A.copy(out=cin[j][:, 3:3 + Ttok], in_=ps[bk][:, :Ttok]), r=[PB[bk]], wa=[B_cin[j]])
                        fw.op(pool, lambda: G.tensor_copy(out=hist[:, ch, :], in_=cin[j][:, Ttok:Ttok + 3]), r=[B_cin[j]], wa=[B_hist])
                        cw_ = pp[:, PP_CONVW + ch * 4:PP_CONVW + ch * 4 + 4]
                        fw.op(act, lambda: A.activation(out=acc[j][:, :Ttok], in_=cin[j][:, 0:Ttok], func=AF.Identity, scale=cw_[:, 0:1],
                                                        bias=pp[:, PP_CONVB + ch:PP_CONVB + ch + 1]),
                              r=[B_cin[j], B_pp], w=[B_acc[j]])
                        for tap in range(1, 4):
                            fw.op(dve, lambda: V.scalar_tensor_tensor(out=acc[j][:, :Ttok], in0=cin[j][:, tap:tap + Ttok], scalar=cw_[:, tap:tap + 1],
                                                                      in1=acc[j][:, :Ttok], op0=ALU.mult, op1=ALU.add),
                                  r=[B_cin[j]], w=[B_acc[j]])
                        k3 = ch % 3
                        if ch < 16:
                            def c3(k3=k3, ch=ch):
                                bt = nb()
                                psb = ps[bt][:].bitcast(BF16)

                                def f():
                                    ins = None
                                    for s in range(nsub):
                                        ins = PE.transpose(psb[:cw, s * 128:(s + 1) * 128], xsa[k3][:, s * cw:(s + 1) * cw], identb[:])
                                    return ins
                                fw.op(pe, f, r=[B_xsa[k3], B_identb], w=[PB[bt]])
                                fw.op(dve, lambda: V.tensor_copy(out=xstok[:cw, :, ch * 128:(ch + 1) * 128],
                                                                 in_=psb[:cw, :nsub * 128].rearrange("p (s c) -> p s c", s=nsub)),
                                      r=[PB[bt]], wa=[B_xstok])

                            def c2(j=j, k3=k3, c3=c3):
                                fw.op(act, lambda: A.activation(out=xsa[k3][:, :Ttok], in_=acc[j][:, :Ttok], func=AF.Silu), r=[B_acc[j]], w=[B_xsa[k3]])
                                pend3.append(c3)
                        else:
                            def c2(j=j, ch=ch):
                                fw.op(act, lambda: A.activation(out=bct[:, ch - 16, :Ttok], in_=acc[j][:, :Ttok], func=AF.Silu), r=[B_acc[j]], wa=[B_bct])
                        pend.append(c2)
                        if len(pend) > 1:
                            pend.pop(0)()
                        if len(pend3) > 1:
                            pend3.pop(0)()
                    blk.done()
                while pend:
                    pend.pop(0)()
                while pend3:
                    pend3.pop(0)()
                blk = Blk()
                wv = blk.t[:, 0:256].rearrange("p (k c) -> p k c", k=8)
                for s in range(nsub):
                    bk = nb()

                    def f():
                        ins = None
                        for kk in range(8):
                            ins = PE.matmul(ps[bk][:cw, 0:32], xnT[:, kk, s * cw:(s + 1) * cw], wv[:, kk, :], start=(kk == 0), stop=(kk == 7))
                        return ins
                    fw.op(pe, f, r=[blk.b, B_xnT], w=[PB[bk]])
                    fw.op(dve, lambda: V.tensor_tensor(out=dtt[:cw, s, :], in0=ps[bk][:cw, 0:32], in1=pp[:cw, PP_DTB:PP_DTB + 32], op=ALU.add),
                          r=[PB[bk], B_pp], wa=[B_dtt])
                blk.done()
                fw.op(act, lambda: A.activation(out=dtt[:cw], in_=dtt[:cw], func=AF.Exp), r=[B_dtt], w=[B_dtt])
                fw.op(act, lambda: A.activation(out=dtt[:cw], in_=dtt[:cw], func=AF.Ln, bias=1.0), r=[B_dtt], w=[B_dtt])
                fw.op(dve, lambda: V.tensor_tensor(out=att[:cw], in0=dtt[:cw], in1=bc(A_rep[:cw], 1, [cw, nsub, 32]), op=ALU.mult),
                      r=[B_dtt, B_small], w=[B_att])

                def chunkprep(s):
                    smc = sm2[s % 2]
                    Bs = B_sm2[s % 2]
                    bk = nb()

                    def f():
                        PE.matmul(ps[bk][:cw, 0:32], Um[:cw, :cw], att[:cw, s, :], start=True, stop=True)
                        return PE.matmul(ps[bk][:, 32:64], ones_f[:cw, :], att[:cw, s, :], start=True, stop=True)
                    fw.op(pe, f, r=[B_att, B_cst], w=[PB[bk]])
                    fw.op(act, lambda: A.copy(out=smc[:cw, 0, :], in_=ps[bk][:cw, 0:32]), r=[PB[bk]], w=[Bs])
                    fw.op(act, lambda: A.activation(out=smc[:cw, 1, :], in_=ps[bk][:cw, 0:32], func=AF.Exp), r=[PB[bk]], w=[Bs])
                    fw.op(act, lambda: A.activation(out=smc[:, 2, :], in_=ps[bk][:, 32:64], func=AF.Exp), r=[PB[bk]], w=[Bs])
                    fw.op(dve, lambda: V.tensor_tensor(out=smc[:cw, 3, :], in0=ps[bk][:cw, 32:64], in1=smc[:cw, 0, :], op=ALU.subtract),
                          r=[PB[bk], Bs], w=[Bs])
                    fw.op(act, lambda: A.activation(out=smc[:cw, 3, :], in_=smc[:cw, 3, :], func=AF.Exp), r=[Bs], w=[Bs])
                    fw.op(dve, lambda: V.tensor_tensor(out=smc[:cw, 4, :], in0=smc[:cw, 3, :], in1=dtt[:cw, s, :], op=ALU.mult),
                          r=[Bs, B_dtt], w=[Bs])

                UI = {}

                def info(u):
                    if u not in UI:
                        s_, g_ = divmod(u, 4)
                        UI[u] = dict(s=s_, g=g_, j=u % 2, k3=u % 3, tsl=slice(s_ * cw, (s_ + 1) * cw),
                                     gsl=slice(g_ * 512, (g_ + 1) * 512), hsl=slice(g_ * 8, (g_ + 1) * 8),
                                     smc=sm2[s_ % 2], Bs=B_sm2[s_ % 2])
                    return UI[u]
                nE = 8 * cw

                def A_rhs(u):
                    d = info(u); j = d["j"]
                    fw.op(pool, lambda: G.tensor_tensor(out=rhsA[j][:cw, :nE].rearrange("p (r i) -> p r i", r=8),
                                                        in0=bc(Um[:cw, :cw], 1, [cw, 8, cw]), in1=bc(att[:cw, d["s"], d["hsl"]], 2, [cw, 8, cw]), op=ALU.mult),
                          r=[B_att, B_cst], w=[B_rhsA[j]])

                def A_pe1(u):
                    d = info(u); j = d["j"]; k3 = d["k3"]; g = d["g"]; tsl = d["tsl"]
                    d["bc"] = nb()
                    fw.op(pe, lambda: PE.matmul(ps[d["bc"]][:cw, :cw], bct[:, g, tsl], bct[:, 4 + g, tsl], start=True, stop=True),
                          r=[B_bct], w=[PB[d["bc"]]])
                    d["bb"] = nb()
                    psb = ps[d["bb"]][:].bitcast(BF16)
                    fw.op(pe, lambda: PE.transpose(psb[:cw, 0:128], bct[:, g, tsl], identb[:]), r=[B_bct, B_identb], w=[PB[d["bb"]]])

                def A_dve1(u):
                    d = info(u); j = d["j"]; k3 = d["k3"]; g = d["g"]; s_ = d["s"]; hsl = d["hsl"]
                    xg = xstok[:cw, s_, d["gsl"]].rearrange("p (r c) -> p r c", r=8)
                    fw.op(dve, lambda: V.tensor_tensor(out=dxt[j][:cw, :].rearrange("p (r c) -> p r c", r=8), in0=xg,
                                                       in1=bc(dtt[:cw, s_, hsl], 2, [cw, 8, 64]), op=ALU.mult),
                          r=[B_xstok, B_dtt], w=[B_dxt[j]])
                    fw.op(dve, lambda: V.tensor_tensor(out=wdt[k3][:cw, :].rearrange("p (r c) -> p r c", r=8), in0=xg,
                                                       in1=bc(d["smc"][:cw, 4, hsl], 2, [cw, 8, 64]), op=ALU.mult),
                          r=[B_xstok, d["Bs"]], w=[B_wdt[k3]])
                    fw.op(dve, lambda: V.tensor_tensor(out=t3[k3][:cw, :].rearrange("p (r c) -> p r c", r=8), in0=xg,
                                                       in1=bc(pp[:cw, PP_DSK + g * 8:PP_DSK + g * 8 + 8], 2, [cw, 8, 64]), op=ALU.mult),
                          r=[B_xstok, B_pp], w=[B_t3[k3]])

                def A_act1(u):
                    d = info(u); k3 = d["k3"]
                    psb = ps[d["bb"]][:].bitcast(BF16)
                    fw.op(act, lambda: A.copy(out=btok[k3][:cw, :], in_=psb[:cw, 0:128]), r=[PB[d["bb"]]], w=[B_btok[k3]])

                def A_seg(u):
                    d = info(u); j = d["j"]
                    nbk = (nE + 511) // 512
                    sbk = [nb() for _ in range(nbk)]
                    for q in range(nbk):
                        fw.op(pe, lambda: PE.matmul(ps[sbk[q]][:cw, :min(512, nE)], Lm[:cw, :cw], rhsA[j][:cw, q * 512:q * 512 + min(512, nE)], start=True, stop=True),
                              r=[B_rhsA[j], B_cst], w=[PB[sbk[q]]])
                        fw.op(act, lambda: A.activation(out=Et[j][:cw, q * 512:q * 512 + min(512, nE)], in_=ps[sbk[q]][:cw, :min(512, nE)], func=AF.Exp),
                              r=[PB[sbk[q]]], w=[B_Et[j]] if q == 0 else (), wa=[B_Et[j]] if q else ())

                def A_dve2(u):
                    d = info(u); j = d["j"]
                    fw.op(dve, lambda: V.tensor_tensor(out=cbm[j][:cw, :cw], in0=ps[d["bc"]][:cw, :cw], in1=Um[:cw, :cw], op=ALU.mult),
                          r=[PB[d["bc"]], B_cst], w=[B_cbm[j]])
                    fw.op(dve, lambda: V.tensor_tensor(out=Mt[j][:cw, :nE].rearrange("p (r i) -> p r i", r=8),
                                                       in0=Et[j][:cw, :nE].rearrange("p (r i) -> p r i", r=8),
                                                       in1=bc(cbm[j][:cw, :cw], 1, [cw, 8, cw]), op=ALU.mult),
                          r=[B_Et[j], B_cbm[j]], w=[B_Mt[j]])

                def A_y(u):
                    d = info(u); j = d["j"]
                    by = 5 + d["k3"]
                    d["by"] = by

                    def f():
                        ins = None
                        for r_ in range(8):
                            ins = PE.matmul(ps[by][:cw, r_ * 64:(r_ + 1) * 64], Mt[j][:cw, r_ * cw:(r_ + 1) * cw], dxt[j][:cw, r_ * 64:(r_ + 1) * 64],
                                            start=True, stop=True)
                        return ins
                    fw.op(pe, f, r=[B_Mt[j], B_dxt[j]], w=[PB[by]])

                def B_pe(u):
                    d = info(u); g = d["g"]; k3 = d["k3"]
                    d["bo"] = nb()
                    fw.op(pe, lambda: PE.matmul(ps[d["bo"]][:cw, :], bct[:, 4 + g, d["tsl"]], hbf[:, d["gsl"]], start=True, stop=True),
                          r=[B_bct, B_hbf[g]], w=[PB[d["bo"]]])
                    d["bh"] = nb()
                    fw.op(pe, lambda: PE.matmul(ps[d["bh"]][:, :], btok[k3][:cw, :], wdt[k3][:cw, :], start=True, stop=True),
                          r=[B_btok[k3], B_wdt[k3]], w=[PB[d["bh"]]])

                def B_dve1(u):
                    d = info(u); j = d["j"]
                    fw.op(dve, lambda: V.tensor_tensor(out=t1[j][:cw, :].rearrange("p (r c) -> p r c", r=8),
                                                       in0=ps[d["bo"]][:cw, :].rearrange("p (r c) -> p r c", r=8),
                                                       in1=bc(d["smc"][:cw, 1, d["hsl"]], 2, [cw, 8, 64]), op=ALU.mult),
                          r=[PB[d["bo"]], d["Bs"]], w=[B_t1[j]])
                    fw.op(dve, lambda: V.tensor_tensor(out=t1[j][:cw, :], in0=ps[d["by"]][:cw, :], in1=t1[j][:cw, :], op=ALU.add),
                          r=[PB[d["by"]], B_t1[j]], w=[B_t1[j]])

                def B_pool1(u):
                    d = info(u); j = d["j"]; g = d["g"]
                    fw.op(pool, lambda: G.tensor_tensor(out=hdec[j][:, :].rearrange("p (r c) -> p r c", r=8),
                                                        in0=hst[:, d["gsl"]].rearrange("p (r c) -> p r c", r=8),
                                                        in1=bc(d["smc"][:, 2, d["hsl"]], 2, [128, 8, 64]), op=ALU.mult),
                          r=[B_h[g], d["Bs"]], w=[B_hdec[j]])

                def B_h_(u):
                    d = info(u); j = d["j"]; g = d["g"]; gsl = d["gsl"]
                    fw.op(dve, lambda: V.tensor_tensor(out=hst[:, gsl], in0=hdec[j][:, :], in1=ps[d["bh"]][:, :], op=ALU.add),
                          r=[B_hdec[j], PB[d["bh"]]], w=[B_h[g]])
                    fw.op(act, lambda: A.copy(out=hbf[:, gsl], in_=hst[:, gsl]), r=[B_h[g]], w=[B_hbf[g]])

                def B_fin(u):
                    d = info(u); j = d["j"]; k3 = d["k3"]
                    fw.op(dve, lambda: V.tensor_tensor(out=t3[k3][:cw, :], in0=t3[k3][:cw, :], in1=t1[j][:cw, :], op=ALU.add),
                          r=[B_t1[j], B_t3[k3]], w=[B_t3[k3]])
                    fw.op(pool, lambda: G.tensor_tensor(out=ytok[:cw, d["gsl"]], in0=t3[k3][:cw, :], in1=zact[:cw, d["s"], d["gsl"]], op=ALU.mult),
                          r=[B_t3[k3], B_zact], wa=[B_ytok])

                def post(s):
                    tsl = slice(s * cw, (s + 1) * cw)
                    fw.op(act, lambda: A.activation(out=yn[:cw, :], in_=ytok[:cw, :], func=AF.Square, accum_out=ssq[:cw, 0:1]),
                          r=[B_ytok], w=[B_yn, B_ssq])
                    fw.op(act, lambda: A.activation(out=ssq[:cw, 1:2], in_=ssq[:cw, 0:1], func=AF.Sqrt, scale=1.0 / 2048, bias=EPS),
                          r=[B_ssq], w=[B_ssq])
                    fw.op(dve, lambda: V.reciprocal(out=ssq[:cw, 2:3], in_=ssq[:cw, 1:2]), r=[B_ssq], w=[B_ssq])
                    fw.op(dve, lambda: V.tensor_scalar(out=yn[:cw, :], in0=ytok[:cw, :], scalar1=ssq[:cw, 2:3], scalar2=None, op0=ALU.mult),
                          r=[B_ytok, B_ssq], w=[B_yn])
                    for q4 in range(4):
                        bt = nb()
                        psb = ps[bt][:].bitcast(BF16)

                        def f():
                            ins = None
                            for q in range(4):
                                kc = q4 * 4 + q
                                ins = PE.transpose(psb[:, q * cw:(q + 1) * cw], yn[:cw, kc * 128:(kc + 1) * 128], identb[:cw, :cw])
                            return ins
                        fw.op(pe, f, r=[B_yn, B_identb], w=[PB[bt]])
                        fw.op(dve, lambda: V.tensor_tensor(out=ynT[:, q4 * 4:q4 * 4 + 4, tsl],
                                                           in0=psb[:, 0:4 * cw].rearrange("p (q c) -> p q c", q=4),
                                                           in1=bc(pp[:, PP_SSDNW + q4 * 4:PP_SSDNW + q4 * 4 + 4], 2, [128, 4, cw]), op=ALU.mult),
                              r=[PB[bt], B_pp], wa=[B_ynT])

                NU = nsub * 4
                pstate["nbanks"] = 5
                pstate["bank"] = 0
                prepped = set()

                def ensure_prep(u):
                    s_ = u // 4
                    if u < NU and s_ not in prepped:
                        prepped.add(s_)
                        chunkprep(s_)
                for i in range(-3, NU):
                    ua = i + 2
                    ur = i + 3
                    if 0 <= ur < NU:
                        A_rhs(ur)
                    if 0 <= ua < NU:
                        ensure_prep(ua)
                    if 0 <= i < NU:
                        B_pe(i)
                    if 0 <= ua < NU:
                        A_pe1(ua)
                    if 0 <= i < NU:
                        B_dve1(i)
                        B_pool1(i)
                    if 0 <= ua < NU:
                        A_dve1(ua)
                        A_act1(ua)
                        A_seg(ua)
                    if 0 <= i < NU:
                        B_h_(i)
                        B_fin(i)
                    if 0 <= ua < NU:
                        A_dve2(ua)
                        A_y(ua)
                    if 0 <= i < NU and i % 4 == 3:
                        post(i // 4)
                pstate["nbanks"] = 8
                for b in range(4):
                    blk = Blk()
                    wv = blk.t[:].rearrange("p (m k c) -> p m k c", m=2, k=16)
                    for m in range(2):
                        mc = 2 * b + m
                        bk = gemm_F(lambda kk: wv[:, m, kk, :], 16, lambda kk: ynT[:, kk, :Ttok], Ttok, [blk.b, B_ynT])
                        fw.op(dve, lambda: V.tensor_tensor(out=xT[:, mc, :Ttok], in0=xT[:, mc, :Ttok], in1=ps[bk][:, :Ttok], op=ALU.add),
                              r=[PB[bk], B_xT], wa=[B_xT])
                    blk.done()
                fw.barrier()

        def layer1(Ttok, cw, seq, k0, rope_ap, ok_ap, ov_ap):
            nsub = Ttok // cw
            nprev = k0 // 128
            with contextlib.ExitStack() as ph:
                QT = sb("QT", [128, 8, T], BF16, ph)
                rp = sb("rp", [128, nsub, 128], F32, ph)
                qraw = [sb(f"qraw{i}", [128, 512], F32, ph) for i in range(4)]
                rtmp = [sb(f"rtmp{i}", [128, 512], F32, ph) for i in range(4)]
                kr = [sb(f"kr{i}", [128, 512], F32, ph) for i in range(4)]
                qb = [sb(f"qb{i}", [128, 512], BF16, ph) for i in range(4)]
                kts = [sb(f"kts{i}", [128, 4, 128], BF16, ph) for i in range(4)]
                vb = [sb(f"vb{i}", [128, 512], BF16, ph) for i in range(4)]
                KTh = [sb(f"KTh{i}", [128, LK], BF16, ph) for i in range(2)]
                Vh = [sb(f"Vh{i}", [128, NBK, 130], BF16, ph) for i in range(2)]
                Pt = [sb(f"Pt{i}", [128, 512], BF16, ph) for i in range(4)]
                osm = sb("osm", [128, 8], F32, ph)
                ot0 = [sb(f"ot0{i}", [128, 128], F32, ph) for i in range(2)]
                ot1 = [sb(f"ot1{i}", [128, 128], F32, ph) for i in range(2)]
                ocp = [sb(f"ocp{i}", [128, 4, 258], F32, ph) for i in range(2)]
                of32 = sb("of32", [128, nsub, 8, 128], F32, ph)
                ossq = sb("ossq", [128, 96], F32, ph)
                on = sb("on", [128, nsub, 8, 128], BF16, ph)
                oT = sb("oT", [128, 8, T], BF16, ph)
                Bd = lambda n: fw.buf(n, dma=True)
                Bn = lambda n: fw.buf(n)
                B_QT, B_rp, B_on, B_oT, B_osm = Bn("QT"), Bd("rp"), Bn("on"), Bn("oT"), Bn("osm")
                B_ocp = [Bn("ocp0"), Bn("ocp1")]
                B_of32, B_ossq = Bn("of32"), Bn("ossq")
                B_qraw = [Bn(f"qraw{i}") for i in range(4)]
                B_rtmp = [Bn(f"rtmp{i}") for i in range(4)]
                B_kr = [Bd(f"kr{i}") for i in range(4)]
                B_qb = [Bn(f"qb{i}") for i in range(4)]
                B_kts = [Bd(f"kts{i}") for i in range(4)]
                B_vb = [Bd(f"vb{i}") for i in range(4)]
                B_KTh = [Bd("KTh0"), Bd("KTh1")]
                B_Vh = [Bd("Vh0"), Bd("Vh1")]
                B_Pt = [Bn(f"Pt{i}") for i in range(4)]
                B_ot0 = [Bn("ot00"), Bn("ot01")]
                B_ot1 = [Bn("ot10"), Bn("ot11")]
                dstreams = [b.stream for b in [B_rp] + B_kr + B_kts + B_vb + B_KTh + B_Vh]

                KX = int(os.environ.get('K_X', '3'))
                for i in range(2 if KX & 1 else 0):
                    fw.op(dve, lambda: V.memset(Vh[i][:, :, 128:130], 1.0), w=[B_Vh[i]])
                for s in range(nsub if KX & 2 else 0):
                    fw.dma(sp, rp[:cw, s, :], rope_ap[s * cw:(s + 1) * cw, :], B_rp, wa=[B_rp])
                rmsnorm_xT(2, Ttok)
                cnt = 0
                qpend = []
                q3 = []
                ND = int(os.environ.get('K_ND', '0'))
                KROPE = int(os.environ.get('K_ROPE', '1'))
                KL1 = int(os.environ.get('K_L1', '9'))
                for b in range(6):
                    blk = Blk()
                    wv = blk.t[:].rearrange("p (k c) -> p k c", k=8)
                    for s in range(nsub if int(os.environ.get('K_Q', '1')) else 0):
                        j = cnt % 4
                        cnt += 1
                        bk = nb()

                        def f():
                            ins = None
                            for kk in range(8):
                                ins = PE.matmul(ps[bk][:cw, :], xnT[:, kk, s * cw:(s + 1) * cw], wv[:, kk, :], start=(kk == 0), stop=(kk == 7))
                            return ins
                        fw.op(pe, f, r=[blk.b, B_xnT], w=[PB[bk]])
                        KC = int(os.environ.get('K_C', '3'))
                        if KC < 3:
                            if KC & 1:
                                fw.op(act, lambda: A.copy(out=kr[j][:cw, :], in_=ps[bk][:cw, :]), r=[PB[bk]], w=[B_kr[j]])
                            if KC & 2:
                                fw.op(dve, lambda: V.tensor_copy(out=vb[j][:cw, :], in_=ps[bk][:cw, :]), r=[PB[bk]], w=[B_vb[j]])
                            continue
                        if b < 4 and KROPE:
                            dst = kr[j] if b >= 2 else rtmp[j]
                            B_dst = B_kr[j] if b >= 2 else B_rtmp[j]
                            fw.op(act, lambda: A.copy(out=qraw[j][:cw, :], in_=ps[bk][:cw, :]), r=[PB[bk]], w=[B_qraw[j]])
                            qv = qraw[j][:cw, :].rearrange("p (h t d) -> p h t d", h=8, t=2)
                            dv = dst[:cw, :].rearrange("p (h t d) -> p h t d", h=8, t=2)
                            fw.op(dve, lambda: V.tensor_tensor(out=dv[:, :, 0, :], in0=qv[:, :, 1, :], in1=bc(rp[:cw, s, 64:96], 1, [cw, 8, 32]), op=ALU.mult),
                                  r=[B_qraw[j], B_rp], w=[B_dst])
                            fw.op(dve, lambda: V.tensor_tensor(out=dv[:, :, 1, :], in0=qv[:, :, 0, :], in1=bc(rp[:cw, s, 96:128], 1, [cw, 8, 32]), op=ALU.mult),
                                  r=[B_qraw[j], B_rp], wa=[B_dst])
                            fw.op(pool, lambda: G.tensor_tensor(out=qraw[j][:cw, :].rearrange("p (h d) -> p h d", h=8),
                                                                in0=qraw[j][:cw, :].rearrange("p (h d) -> p h d", h=8),
                                                                in1=bc(rp[:cw, s, 0:64], 1, [cw, 8, 64]), op=ALU.mult),
                                  r=[B_rp, B_dst], w=[B_qraw[j]])
                            def stage3(j=j, b=b, s=s):
                                bt = nb()
                                psb = ps[bt][:].bitcast(BF16)

                                def f2():
                                    ins = None
                                    for q in range(4):
                                        ins = PE.transpose(psb[:, q * 128:q * 128 + cw], qb[j][:cw, q * 128:(q + 1) * 128], identb[:cw, :cw])
                                    return ins
                                fw.op(pe, f2, r=[B_qb[j], B_identb], w=[PB[bt]])
                                pv = psb[:, 0:512].rearrange("p (q c) -> p q c", q=4)[:, :, :cw]
                                if b < 2:
                                    fw.op(dve, lambda: V.tensor_copy(out=QT[:, 4 * b:4 * b + 4, s * cw:(s + 1) * cw], in_=pv), r=[PB[bt]], wa=[B_QT])
                                else:
                                    hb = 4 * (b - 2)
                                    fw.dma(sp, ok_ap[s * cw:(s + 1) * cw, (b - 2) * 512:(b - 1) * 512], kr[j][:cw, :], B_kr[j], r=[B_kr[j]])
                                    fw.op(dve, lambda: V.tensor_copy(out=kts[j][:, :, :cw], in_=pv), r=[PB[bt]], w=[B_kts[j]])
                                    fw.dma(sp, KT[seq, hb:hb + 4, :, k0 + s * cw:k0 + (s + 1) * cw].rearrange("h p c -> p h c"), kts[j][:, :, :cw], B_kts[j],
                                           r=[B_kts[j]], wa=[B_KT])

                            def stage2(j=j, dst=dst, B_dst=B_dst, stage3=stage3):
                                fw.op(dve, lambda: V.tensor_tensor(out=dst[:cw, :], in0=dst[:cw, :], in1=qraw[j][:cw, :], op=ALU.add),
                                      r=[B_qraw[j]], w=[B_dst])
                                fw.op(act, lambda: A.copy(out=qb[j][:cw, :], in_=dst[:cw, :]), r=[B_dst], w=[B_qb[j]])
                                q3.append(stage3)
                            qpend.append(stage2)
                            if len(qpend) > 1:
                                qpend.pop(0)()
                            if len(q3) > 1:
                                q3.pop(0)()
                        else:
                            while qpend:
                                qpend.pop(0)()
                            while q3:
                                q3.pop(0)()
                            hb = 4 * (b - 4)
                            fw.op(act, lambda: A.copy(out=kr[j][:cw, :], in_=ps[bk][:cw, :]), r=[PB[bk]], w=[B_kr[j]])
                            if ND < 2:
                                fw.dma(sp, ov_ap[s * cw:(s + 1) * cw, (b - 4) * 512:(b - 3) * 512], kr[j][:cw, :], B_kr[j], r=[B_kr[j]])
                            fw.op(pool, lambda: G.tensor_copy(out=vb[j][:cw, :], in_=kr[j][:cw, :]), r=[B_kr[j]], w=[B_vb[j]])
                            kblk = (k0 + s * cw) // 128
                            if ND < 1:
                                fw.dma(sp, VS[seq, hb:hb + 4, 0:cw, kblk, :].rearrange("h p e -> p h e"),
                                       vb[j][:cw, :].rearrange("p (h e) -> p h e", h=4), B_vb[j], r=[B_vb[j]], wa=[B_VS])
                    blk.done()
                while qpend:
                    qpend.pop(0)()
                while q3:
                    q3.pop(0)()

                KL1 = int(os.environ.get('K_L1', '9'))
                nkb = nprev + nsub
                klen = k0 + Ttok
                OBK = [0, 1, 2, 3]
                SBK = [4, 5, 6, 7]
                st_cnt = [0]
                for h in range(8):
                    jh = h % 2
                    fw.dma(sp, KTh[jh][:, :klen], KT[seq, h, :, 0:klen], B_KTh[jh], r=[B_KT], w=[B_KTh[jh]])
                    nfull = klen // 128
                    if nfull:
                        fw.dma(sp, Vh[jh][:, 0:nfull, 0:128], VS[seq, h, :, 0:nfull, :], B_Vh[jh], r=[B_VS], wa=[B_Vh[jh]])
                    if klen % 128:
                        rem = klen % 128
                        fw.dma(sp, Vh[jh][:rem, nfull, 0:128], VS[seq, h, 0:rem, nfull, :], B_Vh[jh], r=[B_VS], wa=[B_Vh[jh]])
                    started = [False] * 4

                    def emit_qk(kb):
                        kw = min(128, klen - kb * 128)
                        qs0 = 0 if kb < nprev else (kb - nprev)
                        q0 = qs0 * cw
                        N = Ttok - q0
                        pts = []
                        for c in range(2):
                            sbk = SBK[st_cnt[0] % 4]
                            pt = st_cnt[0] % 4
                            st_cnt[0] += 1
                            pr = slice(c * 64, (c + 1) * 64)
                            fw.op(pe, lambda: PE.matmul(ps[sbk][:kw, :N], KTh[jh][pr, kb * 128:kb * 128 + kw], QT[pr, h, q0:Ttok], start=True, stop=True),
                                  r=[B_KTh[jh], B_QT], w=[PB[sbk]])
                            fw.op(act, lambda: A.activation(out=Pt[pt][:kw, :N], in_=ps[sbk][:kw, :N], func=AF.Exp, scale=0.125),
                                  r=[PB[sbk]], w=[B_Pt[pt]])
                            if kb >= nprev and cw == 128:
                                fw.op(pool, lambda: G.memset(Pt[pt][64:128, 0:64], 0.0), wa=[B_Pt[pt]], r=[B_Pt[pt]])
                            pts.append(pt)
                        return pts

                    def emit_pv(kb, pts):
                        kw = min(128, klen - kb * 128)
                        qs0 = 0 if kb < nprev else (kb - nprev)
                        q0 = qs0 * cw
                        for c in range(2):
                            pt = pts[c]
                            for half in range((nsub + 1) // 2):
                                qss = [q for q in (2 * half, 2 * half + 1) if q < nsub and q >= qs0]
                                if not qss:
                                    continue
                                ob = OBK[2 * c + half]

                                def f():
                                    ins = None
                                    for q in qss:
                                        st_ = not started[2 * c + half]
                                        started[2 * c + half] = True
                                        ins = PE.matmul(ps[ob][:cw, (q % 2) * 129:(q % 2) * 129 + 129], Pt[pt][:kw, q * cw - q0:(q + 1) * cw - q0],
                                                        Vh[jh][:kw, kb, 0:129], start=st_, stop=(kb == nkb - 1), skip_group_check=True)
                                    return ins
                                fw.op(pe, f, r=[B_Pt[pt], B_Vh[jh]], w=[PB[ob]] if kb == 0 else (), wa=[PB[ob]] if kb else ())

                    pend = emit_qk(0)
                    for kb in range(nkb):
                        nxt = emit_qk(kb + 1) if kb + 1 < nkb else None
                        emit_pv(kb, pend)
                        pend = nxt
                    nhalf = (nsub + 1) // 2
                    for c in range(2):
                        for half in range(nhalf):
                            ob = OBK[2 * c + half]
                            fw.op(dve, lambda: V.tensor_copy(out=ocp[jh][:cw, 2 * c + half, :], in_=ps[ob][:cw, 0:258]),
                                  r=[PB[ob]], w=[B_ocp[jh]] if (c == 0 and half == 0) else (), wa=() if (c == 0 and half == 0) else [B_ocp[jh]])
                    for q in range(nsub):
                        jq = q % 2
                        o0 = ocp[jh][:cw, 0 + q // 2, (q % 2) * 129:(q % 2) * 129 + 129]
                        o1 = ocp[jh][:cw, 2 + q // 2, (q % 2) * 129:(q % 2) * 129 + 129]
                        fw.op(dve, lambda: V.reciprocal(out=osm[:cw, 0:1], in_=o0[:, 128:129]), r=[B_ocp[jh]], w=[B_osm])
                        fw.op(dve, lambda: V.reciprocal(out=osm[:cw, 1:2], in_=o1[:, 128:129]), r=[B_ocp[jh], B_osm], w=[B_osm])
                        fw.op(dve, lambda: V.tensor_tensor(out=osm[:cw, 2:3], in0=osm[:cw, 1:2], in1=negl[:cw, :], op=ALU.mult),
                              r=[B_osm, B_small], w=[B_osm])
                        fw.op(dve, lambda: V.tensor_scalar(out=ot0[jq][:cw, :], in0=o0[:, 0:128], scalar1=osm[:cw, 0:1], scalar2=None, op0=ALU.mult),
                              r=[B_ocp[jh], B_osm], w=[B_ot0[jq]])
                        fw.op(dve, lambda: V.scalar_tensor_tensor(out=of32[:cw, q, h, :], in0=o1[:, 0:128], scalar=osm[:cw, 2:3], in1=ot0[jq][:cw, :],
                                                                  op0=ALU.mult, op1=ALU.add),
                              r=[B_ocp[jh], B_osm, B_ot0[jq]], wa=[B_of32])
                        fw.op(dve, lambda: V.tensor_tensor(out=ot1[jq][:cw, :], in0=of32[:cw, q, h, :], in1=of32[:cw, q, h, :], op=ALU.mult),
                              r=[B_of32], w=[B_ot1[jq]])
                        fw.op(dve, lambda: V.tensor_reduce(out=ossq[:cw, q * 8 + h:q * 8 + h + 1], in_=ot1[jq][:cw, :], axis=mybir.AxisListType.X, op=ALU.add),
                              r=[B_ot1[jq]], wa=[B_ossq])
                nqh = nsub * 8
                fw.op(act, lambda: A.activation(out=ossq[:cw, 32:32 + nqh], in_=ossq[:cw, 0:nqh], func=AF.Sqrt, scale=1.0 / 128, bias=EPS),
                      r=[B_ossq], w=[B_ossq])
                fw.op(dve, lambda: V.reciprocal(out=ossq[:cw, 64:64 + nqh], in_=ossq[:cw, 32:32 + nqh]), r=[B_ossq], w=[B_ossq])
                for q in range(nsub):
                    fw.op(pool if q % 2 else dve,
                          (lambda: G.tensor_tensor(out=on[:cw, q, :, :], in0=of32[:cw, q, :, :], in1=bc(ossq[:cw, 64 + q * 8:64 + q * 8 + 8], 2, [cw, 8, 128]), op=ALU.mult))
                          if q % 2 else
                          (lambda: V.tensor_tensor(out=on[:cw, q, :, :], in0=of32[:cw, q, :, :], in1=bc(ossq[:cw, 64 + q * 8:64 + q * 8 + 8], 2, [cw, 8, 128]), op=ALU.mult)),
                          r=[B_of32, B_ossq], wa=[B_on])
                for s in range(nsub):
                    for hh in range(2):
                        bt = nb()
                        psb = ps[bt][:].bitcast(BF16)

                        def f():
                            ins = None
                            for q in range(4):
                                ins = PE.transpose(psb[:, q * 128:q * 128 + cw], on[:cw, s, hh * 4 + q, :], identb[:cw, :cw])
                            return ins
                        fw.op(pe, f, r=[B_on, B_identb], w=[PB[bt]])
                        pv = psb[:, 0:512].rearrange("p (q c) -> p q c", q=4)[:, :, :cw]
                        fw.op(dve, lambda: V.tensor_scalar(out=oT[:, hh * 4:hh * 4 + 4, s * cw:(s + 1) * cw], in0=pv, scalar1=pp[:, PP_SUBW:PP_SUBW + 1],
                                                           scalar2=(1.0 - LAMBDA_INIT), op0=ALU.mult, op1=ALU.mult),
                              r=[PB[bt], B_pp], wa=[B_oT])
                for b in range(2):
                    blk = Blk()
                    wv = blk.t[:].rearrange("p (m k c) -> p m k c", m=4, k=8)
                    for m in range(4):
                        mc = 4 * b + m
                        bk = gemm_F(lambda kk: wv[:, m, kk, :], 8, lambda kk: oT[:, kk, :Ttok], Ttok, [blk.b, B_oT])
                        fw.op(dve, lambda: V.tensor_tensor(out=xT[:, mc, :Ttok], in0=xT[:, mc, :Ttok], in1=ps[bk][:, :Ttok], op=ALU.add),
                              r=[PB[bk], B_xT], wa=[B_xT])
                    blk.done()
                fw.barrier(dstreams)

        def bnd_prefetch(in_spec, ph):
            if in_spec is None:
                return None
            x_rows_ap, Tn, cwn = in_spec
            nsn = Tn // cwn
            xi4 = sb("xi4", [128, nsn, 1024], F32, ph)
            B_xi = [fw.buf(f"xi4_{i}", dma=True) for i in range(nsn)]
            for s in range(nsn):
                fw.dma(sp, xi4[:cwn, s, :], x_rows_ap[s * cwn:(s + 1) * cwn, :], B_xi[s], w=[B_xi[s]])
            return (xi4, B_xi, nsn, cwn)

        def bnd_finish(out_spec, st, ph):
            B_yo = []
            if out_spec is not None:
                Ttok, cw, oy_ap = out_spec
                yT = sb("yT", [128, 8, T], F32, ph)
                yo = [sb(f"yo{i}", [128, 1024], F32, ph) for i in range(2)]
                B_yo = [fw.buf("yo0", dma=True), fw.buf("yo1", dma=True)]
                nsub = Ttok // cw
                rmsnorm_xT(4, Ttok, out_ap=yT)
                for s in range(nsub):
                    j = s % 2
                    for half in range(2):
                        bk = nb()

                        def f():
                            ins = None
                            for q in range(4):
                                kc = half * 4 + q
                                ins = PE.transpose(ps[bk][:cw, q * 128:(q + 1) * 128], yT[:, kc, s * cw:(s + 1) * cw], ident)
                            return ins
                        fw.op(pe, f, r=[B_xnT, B_cst], w=[PB[bk]])
                        fw.op(act, lambda: A.copy(out=yo[j][:cw, half * 512:(half + 1) * 512], in_=ps[bk][:cw, :]), r=[PB[bk]],
                              w=[B_yo[j]] if half == 0 else (), wa=[B_yo[j]] if half else ())
                    fw.dma(sp, oy_ap[s * cw:(s + 1) * cw, :], yo[j][:cw, :], B_yo[j], r=[B_yo[j]])
            B_xi = []
            if st is not None:
                xi4, B_xi, nsn, cwn = st
                for s in range(nsn):
                    for half in range(2):
                        bk = nb()

                        def f():
                            ins = None
                            for q in range(4):
                                kc = half * 4 + q
                                ins = PE.transpose(ps[bk][:, q * cwn:(q + 1) * cwn], xi4[:cwn, s, kc * 128:(kc + 1) * 128], ident[:cwn, :cwn])
                            return ins
                        fw.op(pe, f, r=[B_xi[s], B_cst], w=[PB[bk]])
                        fw.op(act, lambda: A.copy(out=xT[:, half * 4:half * 4 + 4, s * cwn:(s + 1) * cwn],
                                                  in_=ps[bk][:, :4 * cwn].rearrange("p (q c) -> p q c", q=4)),
                              r=[PB[bk], B_xT], wa=[B_xT])
            return [b.stream for b in B_yo + B_xi]

        def boundary(out_spec, in_spec):
            with contextlib.ExitStack() as ph:
                st = bnd_prefetch(in_spec, ph)
                streams = bnd_finish(out_spec, st, ph)
                fw.barrier(streams)

        def state_out(o_ssm_ap, o_conv_ap):
            with contextlib.ExitStack() as ph:
                hs = sb("hs", [128, 16, 128], F32, ph)
                B_hs = fw.buf("hs", dma=True)
                for q4 in range(4):
                    bk = nb()

                    def f():
                        ins = None
                        for q in range(4):
                            blk_ = q4 * 4 + q
                            ins = PE.transpose(ps[bk][:, q * 128:(q + 1) * 128], hst[:, blk_ * 128:(blk_ + 1) * 128], ident)
                        return ins
                    fw.op(pe, f, r=B_h + [B_cst], w=[PB[bk]])
                    fw.op(act, lambda: A.copy(out=hs[:, q4 * 4:q4 * 4 + 4, :], in_=ps[bk][:, :].rearrange("p (q c) -> p q c", q=4)),
                          r=[PB[bk]], wa=[B_hs])
                fw.dma(sp, o_ssm_ap.rearrange("(b p) n -> p b n", p=128), hs[:], B_hs, r=[B_hs])
                with nc.allow_non_contiguous_dma(reason="tiny conv-state rows"):
                    for k in range(3):
                        fw.dma(sp, o_conv_ap[k].rearrange("(c p) -> p c", p=128), hist[:, :, k], B_hist, r=[B_hist])
                fw.barrier([B_hs.stream, B_hist.stream])

        def state_init_zero():
            fw.op(pool, lambda: G.memset(hist[:], 0.0), w=[B_hist])
            for g in range(4):
                fw.op(pool, lambda: G.memset(hst[:, g * 512:(g + 1) * 512], 0.0), w=[B_h[g]])
                fw.op(pool, lambda: G.memset(hbf[:, g * 512:(g + 1) * 512], 0.0), w=[B_hbf[g]])

        def state_init_sample():
            with contextlib.ExitStack() as ph:
                hs = sb("hs", [128, 16, 128], F32, ph)
                B_hs = fw.buf("hs", dma=True)
                fw.dma(sp, hs[:], sssm.rearrange("(b p) n -> p b n", p=128), B_hs, w=[B_hs])
                with nc.allow_non_contiguous_dma(reason="tiny conv-state rows"):
                    for k in range(3):
                        fw.dma(sp, hist[:, :, k], sconv[k].rearrange("(c p) -> p c", p=128), B_hist, wa=[B_hist], w=())
                for g in range(4):
                    bk = nb()

                    def f():
                        ins = None
                        for q in range(4):
                            blk_ = g * 4 + q
                            ins = PE.transpose(ps[bk][:, q * 128:(q + 1) * 128], hs[:, blk_, :], ident)
                        return ins
                    fw.op(pe, f, r=[B_hs, B_cst], w=[PB[bk]])
                    fw.op(act, lambda: A.copy(out=hst[:, g * 512:(g + 1) * 512], in_=ps[bk][:, :]), r=[PB[bk]], w=[B_h[g]])
                    fw.op(pool, lambda: G.tensor_copy(out=hbf[:, g * 512:(g + 1) * 512], in_=hst[:, g * 512:(g + 1) * 512]), r=[B_h[g]], w=[B_hbf[g]])
                fw.barrier([B_hs.stream, B_hist.stream])

        def cache_ingest(seq):
            with contextlib.ExitStack() as ph:
                cb = [sb(f"cb{i}", [128, 1024], BF16, ph) for i in range(2)]
                kts = [sb(f"ckts{i}", [128, 8, 128], BF16, ph) for i in range(2)]
                B_cb = [fw.buf("cb0", dma=True), fw.buf("cb1", dma=True)]
                B_kts = [fw.buf("ckts0", dma=True), fw.buf("ckts1", dma=True)]
                for blk_ in range(PAST // 128):
                    j = blk_ % 2
                    rows = slice(blk_ * 128, (blk_ + 1) * 128)
                    fw.dma(sp, xin[:, :], ck[rows, :], B_xin, w=[B_xin])
                    fw.op(dve, lambda: V.tensor_copy(out=cb[j][:], in_=xin[:]), r=[B_xin], w=[B_cb[j]])
                    for half in range(2):
                        bt = nb()
                        psb = ps[bt][:].bitcast(BF16)

                        def f():
                            ins = None
                            for q in range(4):
                                hh = half * 4 + q
                                ins = PE.transpose(psb[:, q * 128:(q + 1) * 128], cb[j][:, hh * 128:(hh + 1) * 128], identb[:])
                            return ins
                        fw.op(pe, f, r=[B_cb[j], B_identb], w=[PB[bt]])
                        fw.op(act, lambda: A.copy(out=kts[j][:, half * 4:half * 4 + 4, :], in_=psb[:, 0:512].rearrange("p (q c) -> p q c", q=4)),
                              r=[PB[bt]], w=[B_kts[j]] if half == 0 else (), wa=[B_kts[j]] if half else ())
                    fw.dma(sp, KT[seq, :, :, rows].rearrange("h p c -> p h c"), kts[j][:], B_kts[j], r=[B_kts[j]], wa=[B_KT])
                    fw.dma(sp, xin[:, :], cv[rows, :], B_xin, w=[B_xin])
                    fw.op(dve, lambda: V.tensor_copy(out=cb[j][:], in_=xin[:]), r=[B_xin], w=[B_cb[j]])
                    fw.dma(sp, VS[seq, :, :, blk_, :].rearrange("h p e -> p h e"), cb[j][:].rearrange("p (h e) -> p h e", h=8), B_cb[j],
                           r=[B_cb[j]], wa=[B_VS])
                fw.barrier([b.stream for b in B_cb + B_kts])

        tiles = []
        for seq in range(2):
            for t in range(NT):
                tiles.append((seq, t))
        boundary(None, (xp[0, 0:T, :], T, 128))
        for i, (seq, t) in enumerate(tiles):
            t0 = t * T
            if t == 0:
                state_init_zero()
            layer0(T, 128, seq)
            ffn(0, T)
            layer1(T, 128, seq, t0, ropeP[t0:t0 + T, :], o_kp[seq, t0:t0 + T, :], o_vp[seq, t0:t0 + T, :])
            if i + 1 < len(tiles):
                nseq, nt = tiles[i + 1]
                nxt = (xp[nseq, nt * T:(nt + 1) * T, :], T, 128)
            else:
                nxt = (xs_d, LS, LS)
            ffn(1, T, in_spec=nxt, out_spec=(T, 128, o_yp[seq, t0:t0 + T, :]))
            if t == NT - 1:
                state_out(o_ssmp[seq], o_convp[seq])
        state_init_sample()
        cache_ingest(2)
        layer0(LS, LS, 2)
        ffn(0, LS)
        layer1(LS, LS, 2, PAST, ropeS, o_ks, o_vs)
        ffn(1, LS, in_spec=None, out_spec=(LS, LS, o_ys))
        state_out(o_ssms, o_convs)
        fw.finish(sp)
    return nc


def _wstream(inp):
    blocks = []

    def blk(a):
        o = np.zeros((128, 4096), np.float32)
        a = np.ascontiguousarray(a).reshape(128, -1)
        o[:, :a.shape[1]] = a
        blocks.append(o)
    W3 = inp["ssd_in_proj"][0].reshape(8, 128, 5152)
    for b in range(4):
        blk(W3[:, :, b * 512:(b + 1) * 512].transpose(1, 0, 2))
    for b in range(6):
        blk(W3[:, :, 2048 + b * 512:2048 + (b + 1) * 512].transpose(1, 0, 2))
    blk(W3[:, :, 5120:5152].transpose(1, 0, 2))
    Wo = inp["ssd_out_proj"][0].reshape(16, 128, 8, 128)
    for b in range(4):
        blk(Wo[:, :, 2 * b:2 * b + 2, :].transpose(1, 2, 0, 3))

    def ffn(l):
        Wg = inp["ffn_gate"][l].reshape(8, 128, 22, 128)
        Wu = inp["ffn_up"][l].reshape(8, 128, 22, 128)
        for b in range(11):
            g = Wg[:, :, 2 * b:2 * b + 2, :].transpose(1, 2, 0, 3)
            u = Wu[:, :, 2 * b:2 * b + 2, :].transpose(1, 2, 0, 3)
            blk(np.stack([g, u], axis=2))
        Wd = inp["ffn_down"][l].reshape(22, 128, 8, 128)
        for b in range(8):
            blk(Wd[:, :, b, :].transpose(1, 0, 2))
    ffn(0)
    Wq = inp["attn_qkv"][0].reshape(8, 128, 3072)
    for b in range(6):
        blk(Wq[:, :, b * 512:(b + 1) * 512].transpose(1, 0, 2))
    Wa = inp["attn_out"][0].reshape(8, 128, 8, 128)
    for b in range(2):
        blk(Wa[:, :, 4 * b:4 * b + 4, :].transpose(1, 2, 0, 3))
    ffn(1)
    assert len(blocks) == NBLK
    return np.stack(blocks)


def _rope_table(pos):
    half = 32
    inv = (1.0 / (np.float32(10000.0) ** (np.arange(half, dtype=np.float32) / np.float32(half)))).astype(np.float32)
    ang = pos.astype(np.float32)[:, None] * inv[None, :]
    c = np.cos(ang).astype(np.float32)
    s = np.sin(ang).astype(np.float32)
    return np.concatenate([c, c, -s, s], axis=1).astype(np.float32)


def _consts():
    k = np.arange(128)
    ident = np.eye(128, dtype=np.float32)
    U = (k[:, None] <= k[None, :]).astype(np.float32)
    Lm = (k[:, None] > k[None, :]).astype(np.float32)
    ones = np.ones((128, 128), np.float32)
    return np.concatenate([ident, U, Lm, ones], axis=1)


def _pp(inp):
    pp = np.zeros((128, NPP), np.float32)
    nws = [inp["norm_mix"][0], inp["norm_ffn"][0], inp["norm_mix"][1], inp["norm_ffn"][1], inp["norm_final"]]
    for i, w in enumerate(nws):
        pp[:, PP_NW + i * 8:PP_NW + i * 8 + 8] = np.asarray(w).reshape(8, 128).T
    cw = np.asarray(inp["ssd_conv_w"][0])
    pp[:, PP_CONVW:PP_CONVW + 96] = cw.reshape(4, 24, 128).transpose(2, 1, 0).reshape(128, 96)
    pp[:, PP_CONVB:PP_CONVB + 24] = np.asarray(inp["ssd_conv_b"][0]).reshape(24, 128).T
    pp[:, PP_DTB:PP_DTB + 32] = np.broadcast_to(np.asarray(inp["ssd_dt_bias"][0])[None, :], (128, 32))
    pp[:, PP_ALOG:PP_ALOG + 32] = np.broadcast_to(np.asarray(inp["ssd_a_log"][0])[None, :], (128, 32))
    pp[:, PP_DSK:PP_DSK + 32] = np.broadcast_to(np.asarray(inp["ssd_d"][0])[None, :], (128, 32))
    pp[:, PP_SSDNW:PP_SSDNW + 16] = np.asarray(inp["ssd_norm"][0]).reshape(16, 128).T
    lam = np.stack([inp["attn_lambda_q1"][0], inp["attn_lambda_k1"][0], inp["attn_lambda_q2"][0], inp["attn_lambda_k2"][0]])
    pp[:, PP_LAM:PP_LAM + 256] = np.broadcast_to(np.asarray(lam).reshape(1, 256), (128, 256))
    pp[:, PP_SUBW] = np.asarray(inp["attn_subln"][0])
    return pp


_CACHE = {}


def run(inp, L, LS, PAST, T, n_cores, trace=False):
    key = (L, LS, PAST, T)
    if key not in _CACHE:
        _CACHE[key] = build_program(L, LS, PAST, T)
    nc = _CACHE[key]
    inp = {k: np.asarray(v) for k, v in inp.items()}
    wst = _wstream(inp)
    cst = _consts()
    ropeP = _rope_table(np.arange(L))
    ropeS = _rope_table(PAST + np.arange(LS))
    pp = _pp(inp)
    in_maps = []
    for c in range(n_cores):
        in_maps.append({
            "xp": np.ascontiguousarray(inp["x_prompt"][2 * c:2 * c + 2]),
            "xs": np.ascontiguousarray(inp["x_sample"][c]),
            "sconv": np.ascontiguousarray(inp["state_conv"][0, c]),
            "sssm": np.ascontiguousarray(inp["state_ssm"][0, c].reshape(2048, 128)),
            "ck": np.ascontiguousarray(inp["cache_k"][0, c].reshape(PAST, 1024)),
            "cv": np.ascontiguousarray(inp["cache_v"][0, c].reshape(PAST, 1024)),
            "wst": wst, "cst": cst, "ropeP": ropeP, "ropeS": ropeS, "pp": pp,
        })
    res = run_bass_kernel_spmd(nc, in_maps, core_ids=list(range(n_cores)), trace=trace)
    R = res.results
    cat = lambda n: np.concatenate([r[n] for r in R], axis=0)
    stk = lambda n: np.stack([r[n] for r in R], axis=0)
    nb2 = 2 * n_cores
    outs = (
        cat("o_yp"),
        stk("o_ys"),
        cat("o_convp")[None],
        cat("o_ssmp").reshape(1, nb2, 32, 64, 128),
        cat("o_kp").reshape(1, nb2, L, 8, 2, 64),
        cat("o_vp").reshape(1, nb2, L, 8, 128),
        stk("o_convs")[None],
        stk("o_ssms").reshape(1, n_cores, 32, 64, 128),
        stk("o_ks").reshape(1, n_cores, LS, 8, 2, 64),
        stk("o_vs").reshape(1, n_cores, LS, 8, 128),
    )
    return tuple(np.ascontiguousarray(o, dtype=np.float32) for o in outs), res


def kernel(**inputs):
    L = inputs["x_prompt"].shape[1]
    LS = inputs["x_sample"].shape[1]
    PAST = inputs["cache_k"].shape[2]
    outs, _ = run(inputs, L, LS, PAST, 512, 8)
    return outs
```

```python
import contextlib
import math
import os
import numpy as np
import concourse.bass as bass
import concourse.mybir as mybir
from concourse.bass_utils import run_bass_kernel_spmd

F32 = mybir.dt.float32
BF16 = mybir.dt.bfloat16
AF = mybir.ActivationFunctionType
ALU = mybir.AluOpType

NBLK = 61
EPS = 1e-6
LAMBDA_INIT = 0.8 - 0.6 * math.exp(-0.3 * 1)
NSLOT = 3

PP_NW = 0
PP_CONVW = 40
PP_CONVB = 136
PP_DTB = 160
PP_ALOG = 192
PP_DSK = 224
PP_SSDNW = 256
PP_LAM = 272
PP_SUBW = 528
NPP = 532


class Stream:
    def __init__(self, sem, step, name):
        self.sem = sem
        self.step = step
        self.cnt = 0
        self.name = name


class Buf:
    def __init__(self, name, stream=None):
        self.name = name
        self.ws = {}
        self.rs = {}
        self.stream = stream


class Eng:
    def __init__(self, h, stream):
        self.h = h
        self.st = stream
        self.seen = {}


class FW:
    def __init__(self, nc, es):
        self.nc = nc
        self.es = es
        self.nsem = 0
        self.pe = Eng(nc.tensor, self.new_stream("pe", 1))
        self.act = Eng(nc.scalar, self.new_stream("act", 1))
        self.dve = Eng(nc.vector, self.new_stream("dve", 1))
        self.pool = Eng(nc.gpsimd, self.new_stream("pool", 1))
        self.sp = Eng(nc.sync, self.new_stream("sp", 1))
        self.dma_streams = []
        self.named = {}

    def new_stream(self, name, step):
        sem = self.es.enter_context(self.nc.semaphore(f"s{self.nsem}_{name}"))
        self.nsem += 1
        return Stream(sem, step, name)

    def buf(self, name, dma=False):
        st = None
        if dma:
            st = self.named.get(name)
            if st is None:
                st = self.new_stream(name, 16)
                self.named[name] = st
                self.dma_streams.append(st)
        return Buf(name, st)

    def _deps(self, r, w, wa):
        deps = {}

        def add(d):
            for s, c in d.items():
                if deps.get(s, 0) < c:
                    deps[s] = c
        for b in r:
            add(b.ws)
        for b in w:
            add(b.ws)
            add(b.rs)
        for b in wa:
            add(b.rs)
        return deps

    def _wait(self, eng, deps):
        for s, c in deps.items():
            if eng.seen.get(s, 0) < c:
                eng.h.wait_ge(s.sem, c * s.step)
                eng.seen[s] = c

    def _commit(self, st, r, w, wa):
        c = st.cnt
        for b in r:
            if b.rs.get(st, 0) < c:
                b.rs[st] = c
        for b in w:
            b.ws = {st: c}
            b.rs = {}
        for b in wa:
            if b.ws.get(st, 0) < c:
                b.ws[st] = c

    def op(self, eng, fn, r=(), w=(), wa=()):
        self._wait(eng, self._deps(r, w, wa))
        ins = fn()
        eng.st.cnt += 1
        ins.then_inc(eng.st.sem, 1)
        self._commit(eng.st, r, w, wa)

    def dma(self, q, out, in_, slot, r=(), w=(), wa=(), **kw):
        st = slot.stream
        self._wait(q, self._deps(r, w, wa))
        ins = q.h.dma_start(out=out, in_=in_, **kw)
        st.cnt += 1
        ins.then_inc(st.sem, 16)
        self._commit(st, r, w, wa)

    def barrier(self, streams=()):
        engs = [self.pe, self.act, self.dve, self.pool]
        deps = {e.st: e.st.cnt for e in engs if e.st.cnt > 0}
        for s in streams:
            if s.cnt > 0:
                deps[s] = s.cnt
        for e in engs:
            self._wait(e, {s: c for s, c in deps.items() if s is not e.st})

    def finish(self, eng):
        deps = {s: s.cnt for s in self.dma_streams if s.cnt > 0}
        for e in [self.pe, self.act, self.dve, self.pool]:
            if e.st.cnt > 0:
                deps[e.st] = e.st.cnt
        self._wait(eng, deps)


def build_program(L, LS, PAST, T):
    nc = bass.Bass("TRN2", target_bir_lowering=False)
    V, A, G, PE = nc.vector, nc.scalar, nc.gpsimd, nc.tensor
    NT = L // T
    LK = max(L, PAST + LS)
    NBK = (LK + 127) // 128

    def din(n, s):
        return nc.dram_tensor(n, s, F32, kind="ExternalInput").ap()

    def dout(n, s):
        return nc.dram_tensor(n, s, F32, kind="ExternalOutput").ap()

    xp = din("xp", [2, L, 1024])
    xs_d = din("xs", [LS, 1024])
    sconv = din("sconv", [3, 3072])
    sssm = din("sssm", [2048, 128])
    ck = din("ck", [PAST, 1024])
    cv = din("cv", [PAST, 1024])
    wst = din("wst", [NBLK, 128, 4096])
    cst = din("cst", [128, 512])
    ropeP = din("ropeP", [L, 128])
    ropeS = din("ropeS", [LS, 128])
    pp_d = din("pp", [128, NPP])
    o_yp = dout("o_yp", [2, L, 1024])
    o_ys = dout("o_ys", [LS, 1024])
    o_convp = dout("o_convp", [2, 3, 3072])
    o_ssmp = dout("o_ssmp", [2, 2048, 128])
    o_kp = dout("o_kp", [2, L, 1024])
    o_vp = dout("o_vp", [2, L, 1024])
    o_convs = dout("o_convs", [3, 3072])
    o_ssms = dout("o_ssms", [2048, 128])
    o_ks = dout("o_ks", [LS, 1024])
    o_vs = dout("o_vs", [LS, 1024])
    wbf = nc.dram_tensor("wbf", [NBLK, 128, 4096], BF16, kind="Internal").ap()
    KT = nc.dram_tensor("KTs", [3, 8, 128, LK], BF16, kind="Internal").ap()
    VS = nc.dram_tensor("VSs", [3, 8, 128, NBK, 128], BF16, kind="Internal").ap()

    with contextlib.ExitStack() as es:
        fw = FW(nc, es)
        pe, act, dve, pool, sp = fw.pe, fw.act, fw.dve, fw.pool, fw.sp

        uniq = [0]

        def sb(n, s, d=F32, st=es):
            uniq[0] += 1
            return st.enter_context(nc.sbuf_tensor(f"{n}_{uniq[0]}", s, d))

        cst_t = sb("cst_t", [128, 512])
        ident = cst_t[:, 0:128]
        Um = cst_t[:, 128:256]
        Lm = cst_t[:, 256:384]
        ones_f = cst_t[:, 384:512]
        identb = sb("identb", [128, 128], BF16)
        ones_b = sb("ones_b", [128, 128], BF16)
        pp = sb("pp_t", [128, NPP])
        smallp = sb("smallp", [128, 64])
        xT = sb("xT", [128, 8, T])
        xnT = sb("xnT", [128, 8, T], BF16)
        wring = [sb(f"wring{i}", [128, 4096], BF16) for i in range(NSLOT)]
        xin = sb("xin", [128, 1024])
        sqr = [sb(f"sqr{i}", [128, T], BF16) for i in range(2)]
        rt = sb("rt", [128, T])
        rstd = sb("rstd", [128, T])
        hst = sb("hst", [128, 2048])
        hbf = sb("hbf", [128, 2048], BF16)
        hist = sb("hist", [128, 24, 3])
        ps = [es.enter_context(nc.psum_tensor(f"ps{i}", [128, 512], F32)) for i in range(8)]

        B_cst = fw.buf("cst", dma=True)
        B_pp = fw.buf("pp", dma=True)
        B_identb = fw.buf("identb")
        B_small = fw.buf("small")
        B_xT = fw.buf("xT")
        B_xnT = fw.buf("xnT")
        B_ring = [fw.buf(f"ring{i}", dma=True) for i in range(NSLOT)]
        B_xin = fw.buf("xin", dma=True)
        B_sqr = [fw.buf(f"sqr{i}") for i in range(2)]
        B_rt = fw.buf("rt")
        B_rstd = fw.buf("rstd")
        B_h = [fw.buf(f"h{g}") for g in range(4)]
        B_hbf = [fw.buf(f"hbf{g}") for g in range(4)]
        B_hist = fw.buf("hist", dma=True)
        PB = [fw.buf(f"psb{i}") for i in range(8)]
        B_pro = fw.buf("pro", dma=True)
        B_wbf = fw.buf("wbf")
        B_KT = fw.buf("KT")
        B_VS = fw.buf("VS")
        pstate = {"bank": 0, "g": 0, "nbanks": 8}

        def nb():
            i = pstate["bank"]
            pstate["bank"] = (i + 1) % pstate["nbanks"]
            return i % pstate["nbanks"]

        def bc(ap, axis, shape):
            return ap.unsqueeze(axis).to_broadcast(shape)

        fw.dma(sp, cst_t[:], cst, B_cst, w=[B_cst])
        fw.dma(sp, pp[:], pp_d, B_pp, w=[B_pp])
        PGRP = [0] * 11 + [1] * 23 + [2] * 8 + [3] * 19
        B_prog = [fw.buf(f"pro{i}", dma=True) for i in range(4)]
        B_wbfg = [fw.buf(f"wbf{i}") for i in range(4)]
        for b in range(NBLK):
            fw.dma(pool, wbf[b], wst[b], B_prog[PGRP[b]], wa=[B_wbfg[PGRP[b]]])
        fw.op(dve, lambda: V.tensor_copy(out=identb[:], in_=ident), r=[B_cst], w=[B_identb])
        fw.op(dve, lambda: V.tensor_copy(out=ones_b[:], in_=ones_f), r=[B_cst], wa=[B_identb])
        fw.op(act, lambda: A.activation(out=smallp[:, 0:32], in_=pp[:, PP_ALOG:PP_ALOG + 32], func=AF.Exp),
              r=[B_pp], w=[B_small])
        fw.op(dve, lambda: V.tensor_scalar(out=smallp[:, 0:32], in0=smallp[:, 0:32], scalar1=-1.0, scalar2=None,
                                           op0=ALU.mult), r=[B_small], w=[B_small])
        lq = pp[:, PP_LAM:PP_LAM + 256].rearrange("p (a b) -> p a b", a=4)
        lamt = sb("lamt", [128, 2, 64])
        B_lamt = fw.buf("lamt")
        fw.op(dve, lambda: V.tensor_tensor(out=lamt[:, 0, :], in0=lq[:, 0, :], in1=lq[:, 1, :], op=ALU.mult),
              r=[B_pp], w=[B_lamt])
        fw.op(dve, lambda: V.tensor_tensor(out=lamt[:, 1, :], in0=lq[:, 2, :], in1=lq[:, 3, :], op=ALU.mult),
              r=[B_pp, B_lamt], w=[B_lamt])
        fw.op(dve, lambda: V.tensor_reduce(out=smallp[:, 33:35], in_=lamt[:], axis=mybir.AxisListType.X, op=ALU.add),
              r=[B_lamt, B_small], w=[B_small])
        fw.op(act, lambda: A.activation(out=smallp[:, 35:37], in_=smallp[:, 33:35], func=AF.Exp), r=[B_small], w=[B_small])
        fw.op(dve, lambda: V.scalar_tensor_tensor(out=smallp[:, 32:33], in0=smallp[:, 36:37], scalar=-LAMBDA_INIT,
                                                  in1=smallp[:, 35:36], op0=ALU.add, op1=ALU.subtract),
              r=[B_small], w=[B_small])
        negl = smallp[:, 32:33]
        A_rep = smallp[:, 0:32]

        total_blocks = (2 * NT + 1) * NBLK

        def ring_issue(g):
            if g >= total_blocks or g >= int(os.environ.get('K_RING', '1000000')):
                return
            s = g % NSLOT
            fw.dma(sp, wring[s][:], wbf[g % NBLK], B_ring[s], r=[B_wbfg[PGRP[g % NBLK]]], w=[B_ring[s]])

        for g in range(NSLOT):
            ring_issue(g)

        class Blk:
            def __init__(self):
                self.g = pstate["g"]
                self.s = self.g % NSLOT
                self.t = wring[self.s]
                self.b = B_ring[self.s]

            def done(self):
                pstate["g"] += 1
                ring_issue(self.g + NSLOT)

        def rmsnorm_xT(widx, cw_tot, out_bf=True, out_ap=None):
            bk = nb()
            for kc in range(8):
                j = kc % 2
                fw.op(act, lambda: A.activation(out=sqr[j][:, :cw_tot], in_=xT[:, kc, :cw_tot], func=AF.Square),
                      r=[B_xT], w=[B_sqr[j]])
                fw.op(pe, lambda: PE.matmul(ps[bk][:, :cw_tot], ones_b[:], sqr[j][:, :cw_tot], start=(kc == 0), stop=(kc == 7)),
                      r=[B_sqr[j], B_identb], w=[PB[bk]] if kc == 0 else (), wa=[PB[bk]] if kc else ())
            fw.op(act, lambda: A.activation(out=rt[:, :cw_tot], in_=ps[bk][:, :cw_tot], func=AF.Sqrt, scale=1.0 / 1024, bias=EPS),
                  r=[PB[bk]], w=[B_rt])
            fw.op(dve, lambda: V.reciprocal(out=rstd[:, :cw_tot], in_=rt[:, :cw_tot]), r=[B_rt], w=[B_rstd])
            for kc in range(8):
                dst = xnT[:, kc, :cw_tot] if out_ap is None else out_ap[:, kc, :cw_tot]
                fw.op(dve, lambda: V.scalar_tensor_tensor(out=dst, in0=xT[:, kc, :cw_tot],
                                                          scalar=pp[:, PP_NW + widx * 8 + kc:PP_NW + widx * 8 + kc + 1],
                                                          in1=rstd[:, :cw_tot], op0=ALU.mult, op1=ALU.mult),
                      r=[B_xT, B_rstd, B_pp], wa=[B_xnT])

        def gemm_F(blk_view_fn, nk, rhs_fn, ntok, rbufs):
            bk = nb()

            def f():
                ins = None
                for kk in range(nk):
                    ins = PE.matmul(ps[bk][:, :ntok], blk_view_fn(kk), rhs_fn(kk), start=(kk == 0), stop=(kk == nk - 1))
                return ins
            fw.op(pe, f, r=rbufs, w=[PB[bk]])
            return bk

        def ffn(layer, Ttok, in_spec=None, out_spec=None):
            with contextlib.ExitStack() as ph:
                bst = bnd_prefetch(in_spec, ph)
                hT = sb("hT", [128, 22, T], BF16, ph)
                sg = [sb(f"sg{i}", [128, T], F32, ph) for i in range(2)]
                B_hT = fw.buf("hT")
                B_sg = [fw.buf(f"sg{i}") for i in range(2)]
                rmsnorm_xT(1 + 2 * layer, Ttok)
                for b in range(11):
                    blk = Blk()
                    wv = blk.t[:].rearrange("p (f u k c) -> p f u k c", f=2, u=2, k=8)
                    for f_ in range(2):
                        fc = 2 * b + f_
                        bg = gemm_F(lambda kk: wv[:, f_, 0, kk, :], 8, lambda kk: xnT[:, kk, :Ttok], Ttok, [blk.b, B_xnT])
                        bu = gemm_F(lambda kk: wv[:, f_, 1, kk, :], 8, lambda kk: xnT[:, kk, :Ttok], Ttok, [blk.b, B_xnT])
                        j = fc % 2
                        fw.op(act, lambda: A.activation(out=sg[j][:, :Ttok], in_=ps[bg][:, :Ttok], func=AF.Silu),
                              r=[PB[bg]], w=[B_sg[j]])
                        fw.op(dve, lambda: V.tensor_tensor(out=hT[:, fc, :Ttok], in0=sg[j][:, :Ttok], in1=ps[bu][:, :Ttok], op=ALU.mult),
                              r=[B_sg[j], PB[bu]], wa=[B_hT])
                    blk.done()
                for b in range(8):
                    blk = Blk()
                    wv = blk.t[:, 0:2816].rearrange("p (k c) -> p k c", k=22)
                    bk = gemm_F(lambda kk: wv[:, kk, :], 22, lambda kk: hT[:, kk, :Ttok], Ttok, [blk.b, B_hT])
                    fw.op(dve, lambda: V.tensor_tensor(out=xT[:, b, :Ttok], in0=xT[:, b, :Ttok], in1=ps[bk][:, :Ttok], op=ALU.add),
                          r=[PB[bk], B_xT], wa=[B_xT])
                    blk.done()
                streams = bnd_finish(out_spec, bst, ph) if (in_spec is not None or out_spec is not None) else []
                fw.barrier(streams)

        def load_x(x_rows_ap, Ttok, cw):
            nsub = Ttok // cw
            for s in range(nsub):
                fw.dma(sp, xin[:cw, :], x_rows_ap[s * cw:(s + 1) * cw, :], B_xin, w=[B_xin])
                for half in range(2):
                    bk = nb()

                    def f():
                        ins = None
                        for q in range(4):
                            kc = half * 4 + q
                            ins = PE.transpose(ps[bk][:, q * cw:(q + 1) * cw], xin[:cw, kc * 128:(kc + 1) * 128], ident[:cw, :cw])
                        return ins
                    fw.op(pe, f, r=[B_xin, B_cst], w=[PB[bk]])
                    fw.op(act, lambda: A.copy(out=xT[:, half * 4:half * 4 + 4, s * cw:(s + 1) * cw],
                                              in_=ps[bk][:, :4 * cw].rearrange("p (q c) -> p q c", q=4)),
                          r=[PB[bk], B_xT], wa=[B_xT])

        def layer0(Ttok, cw, seq):
            nsub = Ttok // cw
            with contextlib.ExitStack() as ph:
                zact = sb("zact", [128, nsub, 2048], BF16, ph)
                xstok = sb("xstok", [128, nsub, 2048], BF16, ph)
                bct = sb("bct", [128, 8, T], BF16, ph)
                cin = [sb(f"cin{i}", [128, T + 3], F32, ph) for i in range(2)]
                acc = [sb(f"acc{i}", [128, T], F32, ph) for i in range(2)]
                xsa = [sb(f"xsa{i}", [128, T], BF16, ph) for i in range(3)]
                dtt = sb("dtt", [128, nsub, 32], F32, ph)
                att = sb("att", [128, nsub, 32], F32, ph)
                sm2 = [sb(f"sm{i}", [128, 5, 32], F32, ph) for i in range(2)]
                rhsA = [sb(f"rhsA{i}", [128, 1024], F32, ph) for i in range(2)]
                Et = [sb(f"Et{i}", [128, 1024], BF16, ph) for i in range(2)]
                Mt = [sb(f"Mt{i}", [128, 1024], BF16, ph) for i in range(2)]
                dxt = [sb(f"dxt{i}", [128, 512], BF16, ph) for i in range(2)]
                wdt = [sb(f"wdt{i}", [128, 512], BF16, ph) for i in range(3)]
                cbm = [sb(f"cbm{i}", [128, 128], BF16, ph) for i in range(2)]
                t1 = [sb(f"t1{i}", [128, 512], F32, ph) for i in range(2)]
                t3 = [sb(f"t3{i}", [128, 512], F32, ph) for i in range(3)]
                hdec = [sb(f"hdec{i}", [128, 512], F32, ph) for i in range(2)]
                btok = [sb(f"btok{i}", [128, 128], BF16, ph) for i in range(3)]
                ytok = sb("ytok", [128, 2048], F32, ph)
                yn = sb("yn", [128, 2048], BF16, ph)
                ssq = sb("ssq", [128, 4], F32, ph)
                ynT = sb("ynT", [128, 16, T], BF16, ph)
                Bn = lambda n: fw.buf(n)
                B_zact, B_xstok, B_bct, B_dtt, B_att = Bn("zact"), Bn("xstok"), Bn("bct"), Bn("dtt"), Bn("att")
                B_sm2 = [Bn("sm0"), Bn("sm1")]
                B_cin = [Bn("cin0"), Bn("cin1")]
                B_acc = [Bn("acc0"), Bn("acc1")]
                B_xsa = [Bn("xsa0"), Bn("xsa1"), Bn("xsa2")]
                B_rhsA = [Bn("rhsA0"), Bn("rhsA1")]
                B_Et = [Bn("Et0"), Bn("Et1")]
                B_Mt = [Bn("Mt0"), Bn("Mt1")]
                B_dxt = [Bn("dx0"), Bn("dx1")]
                B_wdt = [Bn("wd0"), Bn("wd1"), Bn("wd2")]
                B_cbm = [Bn("cbm0"), Bn("cbm1")]
                B_t1 = [Bn("t10"), Bn("t11")]
                B_t3 = [Bn("t30"), Bn("t31"), Bn("t32")]
                B_hdec = [Bn("hdec0"), Bn("hdec1")]
                B_btok = [Bn("btok0"), Bn("btok1"), Bn("btok2")]
                B_ytok, B_yn, B_ssq, B_ynT = Bn("ytok"), Bn("yn"), Bn("ssq"), Bn("ynT")

                rmsnorm_xT(0, Ttok)
                for b in range(4):
                    blk = Blk()
                    wv = blk.t[:].rearrange("p (k c) -> p k c", k=8)
                    for s in range(nsub):
                        bk = nb()

                        def f():
                            ins = None
                            for kk in range(8):
                                ins = PE.matmul(ps[bk][:cw, :], xnT[:, kk, s * cw:(s + 1) * cw], wv[:, kk, :], start=(kk == 0), stop=(kk == 7))
                            return ins
                        fw.op(pe, f, r=[blk.b, B_xnT], w=[PB[bk]])
                        fw.op(act, lambda: A.activation(out=zact[:cw, s, b * 512:(b + 1) * 512], in_=ps[bk][:cw, :], func=AF.Silu),
                              r=[PB[bk]], wa=[B_zact])
                    blk.done()
                pend = []
                pend3 = []
                for b in range(6):
                    blk = Blk()
                    wv = blk.t[:].rearrange("p (k c) -> p k c", k=8)
                    for m in range(4):
                        ch = 4 * b + m
                        j = ch % 2
                        bk = gemm_F(lambda kk: wv[:, kk, m * 128:(m + 1) * 128], 8, lambda kk: xnT[:, kk, :Ttok], Ttok, [blk.b, B_xnT])
                        fw.op(pool, lambda: G.tensor_copy(out=cin[j][:, 0:3], in_=hist[:, ch, :]), r=[B_hist], w=[B_cin[j]])
                        fw.op(act, lambda: A.copy(out=cin[j][:, 3:3 + Ttok], in_=ps[bk][:, :Ttok]), r=[PB[bk]], wa=[B_cin[j]])
                        fw.op(pool, lambda: G.tensor_copy(out=hist[:, ch, :], in_=cin[j][:, Ttok:Ttok + 3]), r=[B_cin[j]], wa=[B_hist])
                        cw_ = pp[:, PP_CONVW + ch * 4:PP_CONVW + ch * 4 + 4]
                        fw.op(act, lambda: A.activation(out=acc[j][:, :Ttok], in_=cin[j][:, 0:Ttok], func=AF.Identity, scale=cw_[:, 0:1],
                                                        bias=pp[:, PP_CONVB + ch:PP_CONVB + ch + 1]),
                              r=[B_cin[j], B_pp], w=[B_acc[j]])
                        for tap in range(1, 4):
                            fw.op(dve, lambda: V.scalar_tensor_tensor(out=acc[j][:, :Ttok], in0=cin[j][:, tap:tap + Ttok], scalar=cw_[:, tap:tap + 1],
                                                                      in1=acc[j][:, :Ttok], op0=ALU.mult, op1=ALU.add),
                                  r=[B_cin[j]], w=[B_acc[j]])
                        k3 = ch % 3
                        if ch < 16:
                            def c3(k3=k3, ch=ch):
                                bt = nb()
                                psb = ps[bt][:].bitcast(BF16)

                                def f():
                                    ins = None
                                    for s in range(nsub):
                                        ins = PE.transpose(psb[:cw, s * 128:(s + 1) * 128], xsa[k3][:, s * cw:(s + 1) * cw], identb[:])
                                    return ins
                                fw.op(pe, f, r=[B_xsa[k3], B_identb], w=[PB[bt]])
                                fw.op(dve, lambda: V.tensor_copy(out=xstok[:cw, :, ch * 128:(ch + 1) * 128],
                                                                 in_=psb[:cw, :nsub * 128].rearrange("p (s c) -> p s c", s=nsub)),
                                      r=[PB[bt]], wa=[B_xstok])

                            def c2(j=j, k3=k3, c3=c3):
                                fw.op(act, lambda: A.activation(out=xsa[k3][:, :Ttok], in_=acc[j][:, :Ttok], func=AF.Silu), r=[B_acc[j]], w=[B_xsa[k3]])
                                pend3.append(c3)
                        else:
                            def c2(j=j, ch=ch):
                                fw.op(act, lambda: A.activation(out=bct[:, ch - 16, :Ttok], in_=acc[j][:, :Ttok], func=AF.Silu), r=[B_acc[j]], wa=[B_bct])
                        pend.append(c2)
                        if len(pend) > 1:
                            pend.pop(0)()
                        if len(pend3) > 1:
                            pend3.pop(0)()
                    blk.done()
                while pend:
                    pend.pop(0)()
                while pend3:
                    pend3.pop(0)()
                blk = Blk()
                wv = blk.t[:, 0:256].rearrange("p (k c) -> p k c", k=8)
                for s in range(nsub):
                    bk = nb()

                    def f():
                        ins = None
                        for kk in range(8):
                            ins = PE.matmul(ps[bk][:cw, 0:32], xnT[:, kk, s * cw:(s + 1) * cw], wv[:, kk, :], start=(kk == 0), stop=(kk == 7))
                        return ins
                    fw.op(pe, f, r=[blk.b, B_xnT], w=[PB[bk]])
                    fw.op(dve, lambda: V.tensor_tensor(out=dtt[:cw, s, :], in0=ps[bk][:cw, 0:32], in1=pp[:cw, PP_DTB:PP_DTB + 32], op=ALU.add),
                          r=[PB[bk], B_pp], wa=[B_dtt])
                blk.done()
                fw.op(act, lambda: A.activation(out=dtt[:cw], in_=dtt[:cw], func=AF.Exp), r=[B_dtt], w=[B_dtt])
                fw.op(act, lambda: A.activation(out=dtt[:cw], in_=dtt[:cw], func=AF.Ln, bias=1.0), r=[B_dtt], w=[B_dtt])
                fw.op(dve, lambda: V.tensor_tensor(out=att[:cw], in0=dtt[:cw], in1=bc(A_rep[:cw], 1, [cw, nsub, 32]), op=ALU.mult),
                      r=[B_dtt, B_small], w=[B_att])

                def chunkprep(s):
                    smc = sm2[s % 2]
                    Bs = B_sm2[s % 2]
                    bk = nb()

                    def f():
                        PE.matmul(ps[bk][:cw, 0:32], Um[:cw, :cw], att[:cw, s, :], start=True, stop=True)
                        return PE.matmul(ps[bk][:, 32:64], ones_f[:cw, :], att[:cw, s, :], start=True, stop=True)
                    fw.op(pe, f, r=[B_att, B_cst], w=[PB[bk]])
                    fw.op(act, lambda: A.copy(out=smc[:cw, 0, :], in_=ps[bk][:cw, 0:32]), r=[PB[bk]], w=[Bs])
                    fw.op(act, lambda: A.activation(out=smc[:cw, 1, :], in_=ps[bk][:cw, 0:32], func=AF.Exp), r=[PB[bk]], w=[Bs])
                    fw.op(act, lambda: A.activation(out=smc[:, 2, :], in_=ps[bk][:, 32:64], func=AF.Exp), r=[PB[bk]], w=[Bs])
                    fw.op(dve, lambda: V.tensor_tensor(out=smc[:cw, 3, :], in0=ps[bk][:cw, 32:64], in1=smc[:cw, 0, :], op=ALU.subtract),
                          r=[PB[bk], Bs], w=[Bs])
                    fw.op(act, lambda: A.activation(out=smc[:cw, 3, :], in_=smc[:cw, 3, :], func=AF.Exp), r=[Bs], w=[Bs])
                    fw.op(dve, lambda: V.tensor_tensor(out=smc[:cw, 4, :], in0=smc[:cw, 3, :], in1=dtt[:cw, s, :], op=ALU.mult),
                          r=[Bs, B_dtt], w=[Bs])

                UI = {}

                def info(u):
                    if u not in UI:
                        s_, g_ = divmod(u, 4)
                        UI[u] = dict(s=s_, g=g_, j=u % 2, k3=u % 3, tsl=slice(s_ * cw, (s_ + 1) * cw),
                                     gsl=slice(g_ * 512, (g_ + 1) * 512), hsl=slice(g_ * 8, (g_ + 1) * 8),
                                     smc=sm2[s_ % 2], Bs=B_sm2[s_ % 2])
                    return UI[u]
                nE = 8 * cw

                def A_rhs(u):
                    d = info(u); j = d["j"]
                    fw.op(pool, lambda: G.tensor_tensor(out=rhsA[j][:cw, :nE].rearrange("p (r i) -> p r i", r=8),
                                                        in0=bc(Um[:cw, :cw], 1, [cw, 8, cw]), in1=bc(att[:cw, d["s"], d["hsl"]], 2, [cw, 8, cw]), op=ALU.mult),
                          r=[B_att, B_cst], w=[B_rhsA[j]])

                def A_pe1(u):
                    d = info(u); j = d["j"]; k3 = d["k3"]; g = d["g"]; tsl = d["tsl"]
                    d["bc"] = nb()
                    fw.op(pe, lambda: PE.matmul(ps[d["bc"]][:cw, :cw], bct[:, g, tsl], bct[:, 4 + g, tsl], start=True, stop=True),
                          r=[B_bct], w=[PB[d["bc"]]])
                    d["bb"] = nb()
                    psb = ps[d["bb"]][:].bitcast(BF16)
                    fw.op(pe, lambda: PE.transpose(psb[:cw, 0:128], bct[:, g, tsl], identb[:]), r=[B_bct, B_identb], w=[PB[d["bb"]]])

                def A_dve1(u):
                    d = info(u); j = d["j"]; k3 = d["k3"]; g = d["g"]; s_ = d["s"]; hsl = d["hsl"]
                    xg = xstok[:cw, s_, d["gsl"]].rearrange("p (r c) -> p r c", r=8)
                    fw.op(dve, lambda: V.tensor_tensor(out=dxt[j][:cw, :].rearrange("p (r c) -> p r c", r=8), in0=xg,
                                                       in1=bc(dtt[:cw, s_, hsl], 2, [cw, 8, 64]), op=ALU.mult),
                          r=[B_xstok, B_dtt], w=[B_dxt[j]])
                    fw.op(dve, lambda: V.tensor_tensor(out=wdt[k3][:cw, :].rearrange("p (r c) -> p r c", r=8), in0=xg,
                                                       in1=bc(d["smc"][:cw, 4, hsl], 2, [cw, 8, 64]), op=ALU.mult),
                          r=[B_xstok, d["Bs"]], w=[B_wdt[k3]])
                    fw.op(dve, lambda: V.tensor_tensor(out=t3[k3][:cw, :].rearrange("p (r c) -> p r c", r=8), in0=xg,
                                                       in1=bc(pp[:cw, PP_DSK + g * 8:PP_DSK + g * 8 + 8], 2, [cw, 8, 64]), op=ALU.mult),
                          r=[B_xstok, B_pp], w=[B_t3[k3]])

                def A_act1(u):
                    d = info(u); k3 = d["k3"]
                    psb = ps[d["bb"]][:].bitcast(BF16)
                    fw.op(act, lambda: A.copy(out=btok[k3][:cw, :], in_=psb[:cw, 0:128]), r=[PB[d["bb"]]], w=[B_btok[k3]])

                def A_seg(u):
                    d = info(u); j = d["j"]
                    nbk = (nE + 511) // 512
                    sbk = [nb() for _ in range(nbk)]
                    for q in range(nbk):
                        fw.op(pe, lambda: PE.matmul(ps[sbk[q]][:cw, :min(512, nE)], Lm[:cw, :cw], rhsA[j][:cw, q * 512:q * 512 + min(512, nE)], start=True, stop=True),
                              r=[B_rhsA[j], B_cst], w=[PB[sbk[q]]])
                        fw.op(act, lambda: A.activation(out=Et[j][:cw, q * 512:q * 512 + min(512, nE)], in_=ps[sbk[q]][:cw, :min(512, nE)], func=AF.Exp),
                              r=[PB[sbk[q]]], w=[B_Et[j]] if q == 0 else (), wa=[B_Et[j]] if q else ())

                def A_dve2(u):
                    d = info(u); j = d["j"]
                    fw.op(dve, lambda: V.tensor_tensor(out=cbm[j][:cw, :cw], in0=ps[d["bc"]][:cw, :cw], in1=Um[:cw, :cw], op=ALU.mult),
                          r=[PB[d["bc"]], B_cst], w=[B_cbm[j]])
                    fw.op(dve, lambda: V.tensor_tensor(out=Mt[j][:cw, :nE].rearrange("p (r i) -> p r i", r=8),
                                                       in0=Et[j][:cw, :nE].rearrange("p (r i) -> p r i", r=8),
                                                       in1=bc(cbm[j][:cw, :cw], 1, [cw, 8, cw]), op=ALU.mult),
                          r=[B_Et[j], B_cbm[j]], w=[B_Mt[j]])

                def A_y(u):
                    d = info(u); j = d["j"]
                    by = 5 + d["k3"]
                    d["by"] = by

                    def f():
                        ins = None
                        for r_ in range(8):
                            ins = PE.matmul(ps[by][:cw, r_ * 64:(r_ + 1) * 64], Mt[j][:cw, r_ * cw:(r_ + 1) * cw], dxt[j][:cw, r_ * 64:(r_ + 1) * 64],
                                            start=True, stop=True)
                        return ins
                    fw.op(pe, f, r=[B_Mt[j], B_dxt[j]], w=[PB[by]])

                def B_pe(u):
                    d = info(u); g = d["g"]; k3 = d["k3"]
                    d["bo"] = nb()
                    fw.op(pe, lambda: PE.matmul(ps[d["bo"]][:cw, :], bct[:, 4 + g, d["tsl"]], hbf[:, d["gsl"]], start=True, stop=True),
                          r=[B_bct, B_hbf[g]], w=[PB[d["bo"]]])
                    d["bh"] = nb()
                    fw.op(pe, lambda: PE.matmul(ps[d["bh"]][:, :], btok[k3][:cw, :], wdt[k3][:cw, :], start=True, stop=True),
                          r=[B_btok[k3], B_wdt[k3]], w=[PB[d["bh"]]])

                def B_dve1(u):
                    d = info(u); j = d["j"]
                    fw.op(dve, lambda: V.tensor_tensor(out=t1[j][:cw, :].rearrange("p (r c) -> p r c", r=8),
                                                       in0=ps[d["bo"]][:cw, :].rearrange("p (r c) -> p r c", r=8),
                                                       in1=bc(d["smc"][:cw, 1, d["hsl"]], 2, [cw, 8, 64]), op=ALU.mult),
                          r=[PB[d["bo"]], d["Bs"]], w=[B_t1[j]])
                    fw.op(dve, lambda: V.tensor_tensor(out=t1[j][:cw, :], in0=ps[d["by"]][:cw, :], in1=t1[j][:cw, :], op=ALU.add),
                          r=[PB[d["by"]], B_t1[j]], w=[B_t1[j]])

                def B_pool1(u):
                    d = info(u); j = d["j"]; g = d["g"]
                    fw.op(pool, lambda: G.tensor_tensor(out=hdec[j][:, :].rearrange("p (r c) -> p r c", r=8),
                                                        in0=hst[:, d["gsl"]].rearrange("p (r c) -> p r c", r=8),
                                                        in1=bc(d["smc"][:, 2, d["hsl"]], 2, [128, 8, 64]), op=ALU.mult),
                          r=[B_h[g], d["Bs"]], w=[B_hdec[j]])

                def B_h_(u):
                    d = info(u); j = d["j"]; g = d["g"]; gsl = d["gsl"]
                    fw.op(dve, lambda: V.tensor_tensor(out=hst[:, gsl], in0=hdec[j][:, :], in1=ps[d["bh"]][:, :], op=ALU.add),
                          r=[B_hdec[j], PB[d["bh"]]], w=[B_h[g]])
                    fw.op(act, lambda: A.copy(out=hbf[:, gsl], in_=hst[:, gsl]), r=[B_h[g]], w=[B_hbf[g]])

                def B_fin(u):
                    d = info(u); j = d["j"]; k3 = d["k3"]
                    fw.op(dve, lambda: V.tensor_tensor(out=t3[k3][:cw, :], in0=t3[k3][:cw, :], in1=t1[j][:cw, :], op=ALU.add),
                          r=[B_t1[j], B_t3[k3]], w=[B_t3[k3]])
                    fw.op(pool, lambda: G.tensor_tensor(out=ytok[:cw, d["gsl"]], in0=t3[k3][:cw, :], in1=zact[:cw, d["s"], d["gsl"]], op=ALU.mult),
                          r=[B_t3[k3], B_zact], wa=[B_ytok])

                def post(s):
                    tsl = slice(s * cw, (s + 1) * cw)
                    fw.op(act, lambda: A.activation(out=yn[:cw, :], in_=ytok[:cw, :], func=AF.Square, accum_out=ssq[:cw, 0:1]),
                          r=[B_ytok], w=[B_yn, B_ssq])
                    fw.op(act, lambda: A.activation(out=ssq[:cw, 1:2], in_=ssq[:cw, 0:1], func=AF.Sqrt, scale=1.0 / 2048, bias=EPS),
                          r=[B_ssq], w=[B_ssq])
                    fw.op(dve, lambda: V.reciprocal(out=ssq[:cw, 2:3], in_=ssq[:cw, 1:2]), r=[B_ssq], w=[B_ssq])
                    fw.op(dve, lambda: V.tensor_scalar(out=yn[:cw, :], in0=ytok[:cw, :], scalar1=ssq[:cw, 2:3], scalar2=None, op0=ALU.mult),
                          r=[B_ytok, B_ssq], w=[B_yn])
                    for q4 in range(4):
                        bt = nb()
                        psb = ps[bt][:].bitcast(BF16)

                        def f():
                            ins = None
                            for q in range(4):
                                kc = q4 * 4 + q
                                ins = PE.transpose(psb[:, q * cw:(q + 1) * cw], yn[:cw, kc * 128:(kc + 1) * 128], identb[:cw, :cw])
                            return ins
                        fw.op(pe, f, r=[B_yn, B_identb], w=[PB[bt]])
                        fw.op(dve, lambda: V.tensor_tensor(out=ynT[:, q4 * 4:q4 * 4 + 4, tsl],
                                                           in0=psb[:, 0:4 * cw].rearrange("p (q c) -> p q c", q=4),
                                                           in1=bc(pp[:, PP_SSDNW + q4 * 4:PP_SSDNW + q4 * 4 + 4], 2, [128, 4, cw]), op=ALU.mult),
                              r=[PB[bt], B_pp], wa=[B_ynT])

                NU = nsub * 4
                pstate["nbanks"] = 5
                pstate["bank"] = 0
                prepped = set()

                def ensure_prep(u):
                    s_ = u // 4
                    if u < NU and s_ not in prepped:
                        prepped.add(s_)
                        chunkprep(s_)
                for i in range(-3, NU):
                    ua = i + 2
                    ur = i + 3
                    if 0 <= ur < NU:
                        A_rhs(ur)
                    if 0 <= ua < NU:
                        ensure_prep(ua)
                    if 0 <= i < NU:
                        B_pe(i)
                    if 0 <= ua < NU:
                        A_pe1(ua)
                    if 0 <= i < NU:
                        B_dve1(i)
                        B_pool1(i)
                    if 0 <= ua < NU:
                        A_dve1(ua)
                        A_act1(ua)
                        A_seg(ua)
                    if 0 <= i < NU:
                        B_h_(i)
                        B_fin(i)
                    if 0 <= ua < NU:
                        A_dve2(ua)
                        A_y(ua)
                    if 0 <= i < NU and i % 4 == 3:
                        post(i // 4)
                pstate["nbanks"] = 8
                for b in range(4):
                    blk = Blk()
                    wv = blk.t[:].rearrange("p (m k c) -> p m k c", m=2, k=16)
                    for m in range(2):
                        mc = 2 * b + m
                        bk = gemm_F(lambda kk: wv[:, m, kk, :], 16, lambda kk: ynT[:, kk, :Ttok], Ttok, [blk.b, B_ynT])
                        fw.op(dve, lambda: V.tensor_tensor(out=xT[:, mc, :Ttok], in0=xT[:, mc, :Ttok], in1=ps[bk][:, :Ttok], op=ALU.add),
                              r=[PB[bk], B_xT], wa=[B_xT])
                    blk.done()
                fw.barrier()

        def layer1(Ttok, cw, seq, k0, rope_ap, ok_ap, ov_ap):
            nsub = Ttok // cw
            nprev = k0 // 128
            with contextlib.ExitStack() as ph:
                QT = sb("QT", [128, 8, T], BF16, ph)
                rp = sb("rp", [128, nsub, 128], F32, ph)
                qraw = [sb(f"qraw{i}", [128, 512], F32, ph) for i in range(4)]
                rtmp = [sb(f"rtmp{i}", [128, 512], F32, ph) for i in range(4)]
                kr = [sb(f"kr{i}", [128, 512], F32, ph) for i in range(4)]
                qb = [sb(f"qb{i}", [128, 512], BF16, ph) for i in range(4)]
                kts = [sb(f"kts{i}", [128, 4, 128], BF16, ph) for i in range(4)]
                vb = [sb(f"vb{i}", [128, 512], BF16, ph) for i in range(4)]
                KTh = [sb(f"KTh{i}", [128, LK], BF16, ph) for i in range(2)]
                Vh = [sb(f"Vh{i}", [128, NBK, 130], BF16, ph) for i in range(2)]
                Pt = [sb(f"Pt{i}", [128, 512], BF16, ph) for i in range(4)]
                osm = sb("osm", [128, 8], F32, ph)
                ot0 = [sb(f"ot0{i}", [128, 128], F32, ph) for i in range(2)]
                ot1 = [sb(f"ot1{i}", [128, 128], F32, ph) for i in range(2)]
                ocp = [sb(f"ocp{i}", [128, 4, 258], F32, ph) for i in range(2)]
                of32 = sb("of32", [128, nsub, 8, 128], F32, ph)
                ossq = sb("ossq", [128, 96], F32, ph)
                on = sb("on", [128, nsub, 8, 128], BF16, ph)
                oT = sb("oT", [128, 8, T], BF16, ph)
                Bd = lambda n: fw.buf(n, dma=True)
                Bn = lambda n: fw.buf(n)
                B_QT, B_rp, B_on, B_oT, B_osm = Bn("QT"), Bd("rp"), Bn("on"), Bn("oT"), Bn("osm")
                B_ocp = [Bn("ocp0"), Bn("ocp1")]
                B_of32, B_ossq = Bn("of32"), Bn("ossq")
                B_qraw = [Bn(f"qraw{i}") for i in range(4)]
                B_rtmp = [Bn(f"rtmp{i}") for i in range(4)]
                B_kr = [Bd(f"kr{i}") for i in range(4)]
                B_qb = [Bn(f"qb{i}") for i in range(4)]
                B_kts = [Bd(f"kts{i}") for i in range(4)]
                B_vb = [Bd(f"vb{i}") for i in range(4)]
                B_KTh = [Bd("KTh0"), Bd("KTh1")]
                B_Vh = [Bd("Vh0"), Bd("Vh1")]
                B_Pt = [Bn(f"Pt{i}") for i in range(4)]
                B_ot0 = [Bn("ot00"), Bn("ot01")]
                B_ot1 = [Bn("ot10"), Bn("ot11")]
                dstreams = [b.stream for b in [B_rp] + B_kr + B_kts + B_vb + B_KTh + B_Vh]

                KX = int(os.environ.get('K_X', '3'))
                for i in range(2 if KX & 1 else 0):
                    fw.op(dve, lambda: V.memset(Vh[i][:, :, 128:130], 1.0), w=[B_Vh[i]])
                for s in range(nsub if KX & 2 else 0):
                    fw.dma(sp, rp[:cw, s, :], rope_ap[s * cw:(s + 1) * cw, :], B_rp, wa=[B_rp])
                rmsnorm_xT(2, Ttok)
                cnt = 0
                qpend = []
                q3 = []
                ND = int(os.environ.get('K_ND', '0'))
                KROPE = int(os.environ.get('K_ROPE', '1'))
                KL1 = int(os.environ.get('K_L1', '9'))
                for b in range(6):
                    blk = Blk()
                    wv = blk.t[:].rearrange("p (k c) -> p k c", k=8)
                    for s in range(nsub if int(os.environ.get('K_Q', '1')) else 0):
                        j = cnt % 4
                        cnt += 1
                        bk = nb()

                        def f():
                            ins = None
                            for kk in range(8):
                                ins = PE.matmul(ps[bk][:cw, :], xnT[:, kk, s * cw:(s + 1) * cw], wv[:, kk, :], start=(kk == 0), stop=(kk == 7))
                            return ins
                        fw.op(pe, f, r=[blk.b, B_xnT], w=[PB[bk]])
                        KC = int(os.environ.get('K_C', '3'))
                        if KC < 3:
                            if KC & 1:
                                fw.op(act, lambda: A.copy(out=kr[j][:cw, :], in_=ps[bk][:cw, :]), r=[PB[bk]], w=[B_kr[j]])
                            if KC & 2:
                                fw.op(dve, lambda: V.tensor_copy(out=vb[j][:cw, :], in_=ps[bk][:cw, :]), r=[PB[bk]], w=[B_vb[j]])
                            continue
                        if b < 4 and KROPE:
                            dst = kr[j] if b >= 2 else rtmp[j]
                            B_dst = B_kr[j] if b >= 2 else B_rtmp[j]
                            fw.op(act, lambda: A.copy(out=qraw[j][:cw, :], in_=ps[bk][:cw, :]), r=[PB[bk]], w=[B_qraw[j]])
                            qv = qraw[j][:cw, :].rearrange("p (h t d) -> p h t d", h=8, t=2)
                            dv = dst[:cw, :].rearrange("p (h t d) -> p h t d", h=8, t=2)
                            fw.op(dve, lambda: V.tensor_tensor(out=dv[:, :, 0, :], in0=qv[:, :, 1, :], in1=bc(rp[:cw, s, 64:96], 1, [cw, 8, 32]), op=ALU.mult),
                                  r=[B_qraw[j], B_rp], w=[B_dst])
                            fw.op(dve, lambda: V.tensor_tensor(out=dv[:, :, 1, :], in0=qv[:, :, 0, :], in1=bc(rp[:cw, s, 96:128], 1, [cw, 8, 32]), op=ALU.mult),
                                  r=[B_qraw[j], B_rp], wa=[B_dst])
                            fw.op(pool, lambda: G.tensor_tensor(out=qraw[j][:cw, :].rearrange("p (h d) -> p h d", h=8),
                                                                in0=qraw[j][:cw, :].rearrange("p (h d) -> p h d", h=8),
                                                                in1=bc(rp[:cw, s, 0:64], 1, [cw, 8, 64]), op=ALU.mult),
                                  r=[B_rp, B_dst], w=[B_qraw[j]])
                            def stage3(j=j, b=b, s=s):
                                bt = nb()
                                psb = ps[bt][:].bitcast(BF16)

                                def f2():
                                    ins = None
                                    for q in range(4):
                                        ins = PE.transpose(psb[:, q * 128:q * 128 + cw], qb[j][:cw, q * 128:(q + 1) * 128], identb[:cw, :cw])
                                    return ins
                                fw.op(pe, f2, r=[B_qb[j], B_identb], w=[PB[bt]])
                                pv = psb[:, 0:512].rearrange("p (q c) -> p q c", q=4)[:, :, :cw]
                                if b < 2:
                                    fw.op(dve, lambda: V.tensor_copy(out=QT[:, 4 * b:4 * b + 4, s * cw:(s + 1) * cw], in_=pv), r=[PB[bt]], wa=[B_QT])
                                else:
                                    hb = 4 * (b - 2)
                                    fw.dma(sp, ok_ap[s * cw:(s + 1) * cw, (b - 2) * 512:(b - 1) * 512], kr[j][:cw, :], B_kr[j], r=[B_kr[j]])
                                    fw.op(dve, lambda: V.tensor_copy(out=kts[j][:, :, :cw], in_=pv), r=[PB[bt]], w=[B_kts[j]])
                                    fw.dma(sp, KT[seq, hb:hb + 4, :, k0 + s * cw:k0 + (s + 1) * cw].rearrange("h p c -> p h c"), kts[j][:, :, :cw], B_kts[j],
                                           r=[B_kts[j]], wa=[B_KT])

                            def stage2(j=j, dst=dst, B_dst=B_dst, stage3=stage3):
                                fw.op(dve, lambda: V.tensor_tensor(out=dst[:cw, :], in0=dst[:cw, :], in1=qraw[j][:cw, :], op=ALU.add),
                                      r=[B_qraw[j]], w=[B_dst])
                                fw.op(act, lambda: A.copy(out=qb[j][:cw, :], in_=dst[:cw, :]), r=[B_dst], w=[B_qb[j]])
                                q3.append(stage3)
                            qpend.append(stage2)
                            if len(qpend) > 1:
                                qpend.pop(0)()
                            if len(q3) > 1:
                                q3.pop(0)()
                        else:
                            while qpend:
                                qpend.pop(0)()
                            while q3:
                                q3.pop(0)()
                            hb = 4 * (b - 4)
                            fw.op(act, lambda: A.copy(out=kr[j][:cw, :], in_=ps[bk][:cw, :]), r=[PB[bk]], w=[B_kr[j]])
                            if ND < 2:
                                fw.dma(sp, ov_ap[s * cw:(s + 1) * cw, (b - 4) * 512:(b - 3) * 512], kr[j][:cw, :], B_kr[j], r=[B_kr[j]])
                            fw.op(pool, lambda: G.tensor_copy(out=vb[j][:cw, :], in_=kr[j][:cw, :]), r=[B_kr[j]], w=[B_vb[j]])
                            kblk = (k0 + s * cw) // 128
                            if ND < 1:
                                fw.dma(sp, VS[seq, hb:hb + 4, 0:cw, kblk, :].rearrange("h p e -> p h e"),
                                       vb[j][:cw, :].rearrange("p (h e) -> p h e", h=4), B_vb[j], r=[B_vb[j]], wa=[B_VS])
                    blk.done()
                while qpend:
                    qpend.pop(0)()
                while q3:
                    q3.pop(0)()

                KL1 = int(os.environ.get('K_L1', '9'))
                nkb = nprev + nsub
                klen = k0 + Ttok
                OBK = [0, 1, 2, 3]
                SBK = [4, 5, 6, 7]
                st_cnt = [0]
                for h in range(8):
                    jh = h % 2
                    fw.dma(sp, KTh[jh][:, :klen], KT[seq, h, :, 0:klen], B_KTh[jh], r=[B_KT], w=[B_KTh[jh]])
                    nfull = klen // 128
                    if nfull:
                        fw.dma(sp, Vh[jh][:, 0:nfull, 0:128], VS[seq, h, :, 0:nfull, :], B_Vh[jh], r=[B_VS], wa=[B_Vh[jh]])
                    if klen % 128:
                        rem = klen % 128
                        fw.dma(sp, Vh[jh][:rem, nfull, 0:128], VS[seq, h, 0:rem, nfull, :], B_Vh[jh], r=[B_VS], wa=[B_Vh[jh]])
                    started = [False] * 4

                    def emit_qk(kb):
                        kw = min(128, klen - kb * 128)
                        qs0 = 0 if kb < nprev else (kb - nprev)
                        q0 = qs0 * cw
                        N = Ttok - q0
                        pts = []
                        for c in range(2):
                            sbk = SBK[st_cnt[0] % 4]
                            pt = st_cnt[0] % 4
                            st_cnt[0] += 1
                            pr = slice(c * 64, (c + 1) * 64)
                            fw.op(pe, lambda: PE.matmul(ps[sbk][:kw, :N], KTh[jh][pr, kb * 128:kb * 128 + kw], QT[pr, h, q0:Ttok], start=True, stop=True),
                                  r=[B_KTh[jh], B_QT], w=[PB[sbk]])
                            fw.op(act, lambda: A.activation(out=Pt[pt][:kw, :N], in_=ps[sbk][:kw, :N], func=AF.Exp, scale=0.125),
                                  r=[PB[sbk]], w=[B_Pt[pt]])
                            if kb >= nprev and cw == 128:
                                fw.op(pool, lambda: G.memset(Pt[pt][64:128, 0:64], 0.0), wa=[B_Pt[pt]], r=[B_Pt[pt]])
                            pts.append(pt)
                        return pts

                    def emit_pv(kb, pts):
                        kw = min(128, klen - kb * 128)
                        qs0 = 0 if kb < nprev else (kb - nprev)
                        q0 = qs0 * cw
                        for c in range(2):
                            pt = pts[c]
                            for half in range((nsub + 1) // 2):
                                qss = [q for q in (2 * half, 2 * half + 1) if q < nsub and q >= qs0]
                                if not qss:
                                    continue
                                ob = OBK[2 * c + half]

                                def f():
                                    ins = None
                                    for q in qss:
                                        st_ = not started[2 * c + half]
                                        started[2 * c + half] = True
                                        ins = PE.matmul(ps[ob][:cw, (q % 2) * 129:(q % 2) * 129 + 129], Pt[pt][:kw, q * cw - q0:(q + 1) * cw - q0],
                                                        Vh[jh][:kw, kb, 0:129], start=st_, stop=(kb == nkb - 1), skip_group_check=True)
                                    return ins
                                fw.op(pe, f, r=[B_Pt[pt], B_Vh[jh]], w=[PB[ob]] if kb == 0 else (), wa=[PB[ob]] if kb else ())

                    pend = emit_qk(0)
                    for kb in range(nkb):
                        nxt = emit_qk(kb + 1) if kb + 1 < nkb else None
                        emit_pv(kb, pend)
                        pend = nxt
                    nhalf = (nsub + 1) // 2
                    for c in range(2):
                        for half in range(nhalf):
                            ob = OBK[2 * c + half]
                            fw.op(dve, lambda: V.tensor_copy(out=ocp[jh][:cw, 2 * c + half, :], in_=ps[ob][:cw, 0:258]),
                                  r=[PB[ob]], w=[B_ocp[jh]] if (c == 0 and half == 0) else (), wa=() if (c == 0 and half == 0) else [B_ocp[jh]])
                    for q in range(nsub):
                        jq = q % 2
                        o0 = ocp[jh][:cw, 0 + q // 2, (q % 2) * 129:(q % 2) * 129 + 129]
                        o1 = ocp[jh][:cw, 2 + q // 2, (q % 2) * 129:(q % 2) * 129 + 129]
                        fw.op(dve, lambda: V.reciprocal(out=osm[:cw, 0:1], in_=o0[:, 128:129]), r=[B_ocp[jh]], w=[B_osm])
                        fw.op(dve, lambda: V.reciprocal(out=osm[:cw, 1:2], in_=o1[:, 128:129]), r=[B_ocp[jh], B_osm], w=[B_osm])
                        fw.op(dve, lambda: V.tensor_tensor(out=osm[:cw, 2:3], in0=osm[:cw, 1:2], in1=negl[:cw, :], op=ALU.mult),
                              r=[B_osm, B_small], w=[B_osm])
                        fw.op(dve, lambda: V.tensor_scalar(out=ot0[jq][:cw, :], in0=o0[:, 0:128], scalar1=osm[:cw, 0:1], scalar2=None, op0=ALU.mult),
                              r=[B_ocp[jh], B_osm], w=[B_ot0[jq]])
                        fw.op(dve, lambda: V.scalar_tensor_tensor(out=of32[:cw, q, h, :], in0=o1[:, 0:128], scalar=osm[:cw, 2:3], in1=ot0[jq][:cw, :],
                                                                  op0=ALU.mult, op1=ALU.add),
                              r=[B_ocp[jh], B_osm, B_ot0[jq]], wa=[B_of32])
                        fw.op(dve, lambda: V.tensor_tensor(out=ot1[jq][:cw, :], in0=of32[:cw, q, h, :], in1=of32[:cw, q, h, :], op=ALU.mult),
                              r=[B_of32], w=[B_ot1[jq]])
                        fw.op(dve, lambda: V.tensor_reduce(out=ossq[:cw, q * 8 + h:q * 8 + h + 1], in_=ot1[jq][:cw, :], axis=mybir.AxisListType.X, op=ALU.add),
                              r=[B_ot1[jq]], wa=[B_ossq])
                nqh = nsub * 8
                fw.op(act, lambda: A.activation(out=ossq[:cw, 32:32 + nqh], in_=ossq[:cw, 0:nqh], func=AF.Sqrt, scale=1.0 / 128, bias=EPS),
                      r=[B_ossq], w=[B_ossq])
                fw.op(dve, lambda: V.reciprocal(out=ossq[:cw, 64:64 + nqh], in_=ossq[:cw, 32:32 + nqh]), r=[B_ossq], w=[B_ossq])
                for q in range(nsub):
                    fw.op(pool if q % 2 else dve,
                          (lambda: G.tensor_tensor(out=on[:cw, q, :, :], in0=of32[:cw, q, :, :], in1=bc(ossq[:cw, 64 + q * 8:64 + q * 8 + 8], 2, [cw, 8, 128]), op=ALU.mult))
                          if q % 2 else
                          (lambda: V.tensor_tensor(out=on[:cw, q, :, :], in0=of32[:cw, q, :, :], in1=bc(ossq[:cw, 64 + q * 8:64 + q * 8 + 8], 2, [cw, 8, 128]), op=ALU.mult)),
                          r=[B_of32, B_ossq], wa=[B_on])
                for s in range(nsub):
                    for hh in range(2):
                        bt = nb()
                        psb = ps[bt][:].bitcast(BF16)

                        def f():
                            ins = None
                            for q in range(4):
                                ins = PE.transpose(psb[:, q * 128:q * 128 + cw], on[:cw, s, hh * 4 + q, :], identb[:cw, :cw])
                            return ins
                        fw.op(pe, f, r=[B_on, B_identb], w=[PB[bt]])
                        pv = psb[:, 0:512].rearrange("p (q c) -> p q c", q=4)[:, :, :cw]
                        fw.op(dve, lambda: V.tensor_scalar(out=oT[:, hh * 4:hh * 4 + 4, s * cw:(s + 1) * cw], in0=pv, scalar1=pp[:, PP_SUBW:PP_SUBW + 1],
                                                           scalar2=(1.0 - LAMBDA_INIT), op0=ALU.mult, op1=ALU.mult),
                              r=[PB[bt], B_pp], wa=[B_oT])
                for b in range(2):
                    blk = Blk()
                    wv = blk.t[:].rearrange("p (m k c) -> p m k c", m=4, k=8)
                    for m in range(4):
                        mc = 4 * b + m
                        bk = gemm_F(lambda kk: wv[:, m, kk, :], 8, lambda kk: oT[:, kk, :Ttok], Ttok, [blk.b, B_oT])
                        fw.op(dve, lambda: V.tensor_tensor(out=xT[:, mc, :Ttok], in0=xT[:, mc, :Ttok], in1=ps[bk][:, :Ttok], op=ALU.add),
                              r=[PB[bk], B_xT], wa=[B_xT])
                    blk.done()
                fw.barrier(dstreams)

        def bnd_prefetch(in_spec, ph):
            if in_spec is None:
                return None
            x_rows_ap, Tn, cwn = in_spec
            nsn = Tn // cwn
            xi4 = sb("xi4", [128, nsn, 1024], F32, ph)
            B_xi = [fw.buf(f"xi4_{i}", dma=True) for i in range(nsn)]
            for s in range(nsn):
                fw.dma(sp, xi4[:cwn, s, :], x_rows_ap[s * cwn:(s + 1) * cwn, :], B_xi[s], w=[B_xi[s]])
            return (xi4, B_xi, nsn, cwn)

        def bnd_finish(out_spec, st, ph):
            B_yo = []
            if out_spec is not None:
                Ttok, cw, oy_ap = out_spec
                yT = sb("yT", [128, 8, T], F32, ph)
                yo = [sb(f"yo{i}", [128, 1024], F32, ph) for i in range(2)]
                B_yo = [fw.buf("yo0", dma=True), fw.buf("yo1", dma=True)]
                nsub = Ttok // cw
                rmsnorm_xT(4, Ttok, out_ap=yT)
                for s in range(nsub):
                    j = s % 2
                    for half in range(2):
                        bk = nb()

                        def f():
                            ins = None
                            for q in range(4):
                                kc = half * 4 + q
                                ins = PE.transpose(ps[bk][:cw, q * 128:(q + 1) * 128], yT[:, kc, s * cw:(s + 1) * cw], ident)
                            return ins
                        fw.op(pe, f, r=[B_xnT, B_cst], w=[PB[bk]])
                        fw.op(act, lambda: A.copy(out=yo[j][:cw, half * 512:(half + 1) * 512], in_=ps[bk][:cw, :]), r=[PB[bk]],
                              w=[B_yo[j]] if half == 0 else (), wa=[B_yo[j]] if half else ())
                    fw.dma(pool, oy_ap[s * cw:(s + 1) * cw, :], yo[j][:cw, :], B_yo[j], r=[B_yo[j]])
            B_xi = []
            if st is not None:
                xi4, B_xi, nsn, cwn = st
                for s in range(nsn):
                    for half in range(2):
                        bk = nb()

                        def f():
                            ins = None
                            for q in range(4):
                                kc = half * 4 + q
                                ins = PE.transpose(ps[bk][:, q * cwn:(q + 1) * cwn], xi4[:cwn, s, kc * 128:(kc + 1) * 128], ident[:cwn, :cwn])
                            return ins
                        fw.op(pe, f, r=[B_xi[s], B_cst], w=[PB[bk]])
                        fw.op(act, lambda: A.copy(out=xT[:, half * 4:half * 4 + 4, s * cwn:(s + 1) * cwn],
                                                  in_=ps[bk][:, :4 * cwn].rearrange("p (q c) -> p q c", q=4)),
                              r=[PB[bk], B_xT], wa=[B_xT])
            return [b.stream for b in B_yo + B_xi]

        def boundary(out_spec, in_spec):
            with contextlib.ExitStack() as ph:
                st = bnd_prefetch(in_spec, ph)
                streams = bnd_finish(out_spec, st, ph)
                fw.barrier(streams)

        def state_out(o_ssm_ap, o_conv_ap):
            with contextlib.ExitStack() as ph:
                hs = sb("hs", [128, 16, 128], F32, ph)
                B_hs = fw.buf("hs", dma=True)
                for q4 in range(4):
                    bk = nb()

                    def f():
                        ins = None
                        for q in range(4):
                            blk_ = q4 * 4 + q
                            ins = PE.transpose(ps[bk][:, q * 128:(q + 1) * 128], hst[:, blk_ * 128:(blk_ + 1) * 128], ident)
                        return ins
                    fw.op(pe, f, r=B_h + [B_cst], w=[PB[bk]])
                    fw.op(act, lambda: A.copy(out=hs[:, q4 * 4:q4 * 4 + 4, :], in_=ps[bk][:, :].rearrange("p (q c) -> p q c", q=4)),
                          r=[PB[bk]], wa=[B_hs])
                fw.dma(sp, o_ssm_ap.rearrange("(b p) n -> p b n", p=128), hs[:], B_hs, r=[B_hs])
                with nc.allow_non_contiguous_dma(reason="tiny conv-state rows"):
                    for k in range(3):
                        fw.dma(sp, o_conv_ap[k].rearrange("(c p) -> p c", p=128), hist[:, :, k], B_hist, r=[B_hist])
                fw.barrier([B_hs.stream, B_hist.stream])

        def state_init_zero():
            fw.op(pool, lambda: G.memset(hist[:], 0.0), w=[B_hist])
            for g in range(4):
                fw.op(pool, lambda: G.memset(hst[:, g * 512:(g + 1) * 512], 0.0), w=[B_h[g]])
                fw.op(pool, lambda: G.memset(hbf[:, g * 512:(g + 1) * 512], 0.0), w=[B_hbf[g]])

        def state_init_sample():
            with contextlib.ExitStack() as ph:
                hs = sb("hs", [128, 16, 128], F32, ph)
                B_hs = fw.buf("hs", dma=True)
                fw.dma(sp, hs[:], sssm.rearrange("(b p) n -> p b n", p=128), B_hs, w=[B_hs])
                with nc.allow_non_contiguous_dma(reason="tiny conv-state rows"):
                    for k in range(3):
                        fw.dma(sp, hist[:, :, k], sconv[k].rearrange("(c p) -> p c", p=128), B_hist, wa=[B_hist], w=())
                for g in range(4):
                    bk = nb()

                    def f():
                        ins = None
                        for q in range(4):
                            blk_ = g * 4 + q
                            ins = PE.transpose(ps[bk][:, q * 128:(q + 1) * 128], hs[:, blk_, :], ident)
                        return ins
                    fw.op(pe, f, r=[B_hs, B_cst], w=[PB[bk]])
                    fw.op(act, lambda: A.copy(out=hst[:, g * 512:(g + 1) * 512], in_=ps[bk][:, :]), r=[PB[bk]], w=[B_h[g]])
                    fw.op(pool, lambda: G.tensor_copy(out=hbf[:, g * 512:(g + 1) * 512], in_=hst[:, g * 512:(g + 1) * 512]), r=[B_h[g]], w=[B_hbf[g]])
                fw.barrier([B_hs.stream, B_hist.stream])

        def cache_ingest(seq):
            with contextlib.ExitStack() as ph:
                cb = [sb(f"cb{i}", [128, 1024], BF16, ph) for i in range(2)]
                kts = [sb(f"ckts{i}", [128, 8, 128], BF16, ph) for i in range(2)]
                B_cb = [fw.buf("cb0", dma=True), fw.buf("cb1", dma=True)]
                B_kts = [fw.buf("ckts0", dma=True), fw.buf("ckts1", dma=True)]
                for blk_ in range(PAST // 128):
                    j = blk_ % 2
                    rows = slice(blk_ * 128, (blk_ + 1) * 128)
                    fw.dma(sp, xin[:, :], ck[rows, :], B_xin, w=[B_xin])
                    fw.op(dve, lambda: V.tensor_copy(out=cb[j][:], in_=xin[:]), r=[B_xin], w=[B_cb[j]])
                    for half in range(2):
                        bt = nb()
                        psb = ps[bt][:].bitcast(BF16)

                        def f():
                            ins = None
                            for q in range(4):
                                hh = half * 4 + q
                                ins = PE.transpose(psb[:, q * 128:(q + 1) * 128], cb[j][:, hh * 128:(hh + 1) * 128], identb[:])
                            return ins
                        fw.op(pe, f, r=[B_cb[j], B_identb], w=[PB[bt]])
                        fw.op(act, lambda: A.copy(out=kts[j][:, half * 4:half * 4 + 4, :], in_=psb[:, 0:512].rearrange("p (q c) -> p q c", q=4)),
                              r=[PB[bt]], w=[B_kts[j]] if half == 0 else (), wa=[B_kts[j]] if half else ())
                    fw.dma(sp, KT[seq, :, :, rows].rearrange("h p c -> p h c"), kts[j][:], B_kts[j], r=[B_kts[j]], wa=[B_KT])
                    fw.dma(sp, xin[:, :], cv[rows, :], B_xin, w=[B_xin])
                    fw.op(dve, lambda: V.tensor_copy(out=cb[j][:], in_=xin[:]), r=[B_xin], w=[B_cb[j]])
                    fw.dma(sp, VS[seq, :, :, blk_, :].rearrange("h p e -> p h e"), cb[j][:].rearrange("p (h e) -> p h e", h=8), B_cb[j],
                           r=[B_cb[j]], wa=[B_VS])
                fw.barrier([b.stream for b in B_cb + B_kts])

        tiles = []
        for seq in range(2):
            for t in range(NT):
                tiles.append((seq, t))
        boundary(None, (xp[0, 0:T, :], T, 128))
        for i, (seq, t) in enumerate(tiles):
            t0 = t * T
            if t == 0:
                state_init_zero()
            layer0(T, 128, seq)
            ffn(0, T)
            layer1(T, 128, seq, t0, ropeP[t0:t0 + T, :], o_kp[seq, t0:t0 + T, :], o_vp[seq, t0:t0 + T, :])
            if i + 1 < len(tiles):
                nseq, nt = tiles[i + 1]
                nxt = (xp[nseq, nt * T:(nt + 1) * T, :], T, 128)
            else:
                nxt = (xs_d, LS, LS)
            ffn(1, T, in_spec=nxt, out_spec=(T, 128, o_yp[seq, t0:t0 + T, :]))
            if t == NT - 1:
                state_out(o_ssmp[seq], o_convp[seq])
        state_init_sample()
        cache_ingest(2)
        layer0(LS, LS, 2)
        ffn(0, LS)
        layer1(LS, LS, 2, PAST, ropeS, o_ks, o_vs)
        ffn(1, LS, in_spec=None, out_spec=(LS, LS, o_ys))
        state_out(o_ssms, o_convs)
        fw.finish(sp)
    return nc


def _wstream(inp):
    blocks = []

    def blk(a):
        o = np.zeros((128, 4096), np.float32)
        a = np.ascontiguousarray(a).reshape(128, -1)
        o[:, :a.shape[1]] = a
        blocks.append(o)
    W3 = inp["ssd_in_proj"][0].reshape(8, 128, 5152)
    for b in range(4):
        blk(W3[:, :, b * 512:(b + 1) * 512].transpose(1, 0, 2))
    for b in range(6):
        blk(W3[:, :, 2048 + b * 512:2048 + (b + 1) * 512].transpose(1, 0, 2))
    blk(W3[:, :, 5120:5152].transpose(1, 0, 2))
    Wo = inp["ssd_out_proj"][0].reshape(16, 128, 8, 128)
    for b in range(4):
        blk(Wo[:, :, 2 * b:2 * b + 2, :].transpose(1, 2, 0, 3))

    def ffn(l):
        Wg = inp["ffn_gate"][l].reshape(8, 128, 22, 128)
        Wu = inp["ffn_up"][l].reshape(8, 128, 22, 128)
        for b in range(11):
            g = Wg[:, :, 2 * b:2 * b + 2, :].transpose(1, 2, 0, 3)
            u = Wu[:, :, 2 * b:2 * b + 2, :].transpose(1, 2, 0, 3)
            blk(np.stack([g, u], axis=2))
        Wd = inp["ffn_down"][l].reshape(22, 128, 8, 128)
        for b in range(8):
            blk(Wd[:, :, b, :].transpose(1, 0, 2))
    ffn(0)
    Wq = inp["attn_qkv"][0].reshape(8, 128, 3072)
    for b in range(6):
        blk(Wq[:, :, b * 512:(b + 1) * 512].transpose(1, 0, 2))
    Wa = inp["attn_out"][0].reshape(8, 128, 8, 128)
    for b in range(2):
        blk(Wa[:, :, 4 * b:4 * b + 4, :].transpose(1, 2, 0, 3))
    ffn(1)
    assert len(blocks) == NBLK
    return np.stack(blocks)


def _rope_table(pos):
    half = 32
    inv = (1.0 / (np.float32(10000.0) ** (np.arange(half, dtype=np.float32) / np.float32(half)))).astype(np.float32)
    ang = pos.astype(np.float32)[:, None] * inv[None, :]
    c = np.cos(ang).astype(np.float32)
    s = np.sin(ang).astype(np.float32)
    return np.concatenate([c, c, -s, s], axis=1).astype(np.float32)


def _consts():
    k = np.arange(128)
    ident = np.eye(128, dtype=np.float32)
    U = (k[:, None] <= k[None, :]).astype(np.float32)
    Lm = (k[:, None] > k[None, :]).astype(np.float32)
    ones = np.ones((128, 128), np.float32)
    return np.concatenate([ident, U, Lm, ones], axis=1)


def _pp(inp):
    pp = np.zeros((128, NPP), np.float32)
    nws = [inp["norm_mix"][0], inp["norm_ffn"][0], inp["norm_mix"][1], inp["norm_ffn"][1], inp["norm_final"]]
    for i, w in enumerate(nws):
        pp[:, PP_NW + i * 8:PP_NW + i * 8 + 8] = np.asarray(w).reshape(8, 128).T
    cw = np.asarray(inp["ssd_conv_w"][0])
    pp[:, PP_CONVW:PP_CONVW + 96] = cw.reshape(4, 24, 128).transpose(2, 1, 0).reshape(128, 96)
    pp[:, PP_CONVB:PP_CONVB + 24] = np.asarray(inp["ssd_conv_b"][0]).reshape(24, 128).T
    pp[:, PP_DTB:PP_DTB + 32] = np.broadcast_to(np.asarray(inp["ssd_dt_bias"][0])[None, :], (128, 32))
    pp[:, PP_ALOG:PP_ALOG + 32] = np.broadcast_to(np.asarray(inp["ssd_a_log"][0])[None, :], (128, 32))
    pp[:, PP_DSK:PP_DSK + 32] = np.broadcast_to(np.asarray(inp["ssd_d"][0])[None, :], (128, 32))
    pp[:, PP_SSDNW:PP_SSDNW + 16] = np.asarray(inp["ssd_norm"][0]).reshape(16, 128).T
    lam = np.stack([inp["attn_lambda_q1"][0], inp["attn_lambda_k1"][0], inp["attn_lambda_q2"][0], inp["attn_lambda_k2"][0]])
    pp[:, PP_LAM:PP_LAM + 256] = np.broadcast_to(np.asarray(lam).reshape(1, 256), (128, 256))
    pp[:, PP_SUBW] = np.asarray(inp["attn_subln"][0])
    return pp


_CACHE = {}


def run(inp, L, LS, PAST, T, n_cores, trace=False):
    key = (L, LS, PAST, T)
    if key not in _CACHE:
        _CACHE[key] = build_program(L, LS, PAST, T)
    nc = _CACHE[key]
    inp = {k: np.asarray(v) for k, v in inp.items()}
    wst = _wstream(inp)
    cst = _consts()
    ropeP = _rope_table(np.arange(L))
    ropeS = _rope_table(PAST + np.arange(LS))
    pp = _pp(inp)
    in_maps = []
    for c in range(n_cores):
        in_maps.append({
            "xp": np.ascontiguousarray(inp["x_prompt"][2 * c:2 * c + 2]),
            "xs": np.ascontiguousarray(inp["x_sample"][c]),
            "sconv": np.ascontiguousarray(inp["state_conv"][0, c]),
            "sssm": np.ascontiguousarray(inp["state_ssm"][0, c].reshape(2048, 128)),
            "ck": np.ascontiguousarray(inp["cache_k"][0, c].reshape(PAST, 1024)),
            "cv": np.ascontiguousarray(inp["cache_v"][0, c].reshape(PAST, 1024)),
            "wst": wst, "cst": cst, "ropeP": ropeP, "ropeS": ropeS, "pp": pp,
        })
    res = run_bass_kernel_spmd(nc, in_maps, core_ids=list(range(n_cores)), trace=trace)
    R = res.results
    cat = lambda n: np.concatenate([r[n] for r in R], axis=0)
    stk = lambda n: np.stack([r[n] for r in R], axis=0)
    nb2 = 2 * n_cores
    outs = (
        cat("o_yp"),
        stk("o_ys"),
        cat("o_convp")[None],
        cat("o_ssmp").reshape(1, nb2, 32, 64, 128),
        cat("o_kp").reshape(1, nb2, L, 8, 2, 64),
        cat("o_vp").reshape(1, nb2, L, 8, 128),
        stk("o_convs")[None],
        stk("o_ssms").reshape(1, n_cores, 32, 64, 128),
        stk("o_ks").reshape(1, n_cores, LS, 8, 2, 64),
        stk("o_vs").reshape(1, n_cores, LS, 8, 128),
    )
    return tuple(np.ascontiguousarray(o, dtype=np.float32) for o in outs), res


def kernel(**inputs):
    L = inputs["x_prompt"].shape[1]
    LS = inputs["x_sample"].shape[1]
    PAST = inputs["cache_k"].shape[2]
    outs, _ = run(inputs, L, LS, PAST, 512, 8)
    return outs
```
